# Optimizing a Trainium2 kernel written in Bass

```python
import math
import jax, jax.numpy as jnp
from jax import lax
import numpy as np

D_MODEL = 1024
BATCH = 32
SEQ = 256
DEPTH = 2
DEC_BATCH = 4
DEC_SEQ = 4096
PAST_LEN = 256

GRID_W = 64
N_EVEN = (DEPTH + 1) // 2
N_ODD = DEPTH // 2
MLA_HEADS = 8
Q_LORA = 384
KV_LORA = 256
QK_NOPE = 64
QK_ROPE = 32
QK_HEAD = QK_NOPE + QK_ROPE
V_HEAD = 64
ROPE_THETA = 10000.0
Q_BLOCK = 128
GLA_HEADS = 4
GLA_DK = 64
GLA_DV = 128
ALPHA_RANK = 16
GLA_TAU = 16.0
GLA_CHUNK = 64
CONV_WIDTH = 31
CONV_PAD = CONV_WIDTH // 2
FFN_HIDDEN = ((8 * D_MODEL // 3 + 255) // 256) * 256
MIX_OUT = MLA_HEADS * V_HEAD + GLA_HEADS * GLA_DV
IN_SPLITS = (Q_LORA, KV_LORA, QK_ROPE, GLA_HEADS * GLA_DK, GLA_HEADS * GLA_DK,
             GLA_HEADS * GLA_DV, GLA_HEADS * GLA_DV, 2 * ALPHA_RANK)
IN_WIDTH = sum(IN_SPLITS)
EPS = 1e-6

kernel_name = 'hybrid_mla_gla_conformer_dit_step'


def rmsnorm(x, g):
    xf = x.astype(jnp.float32)
    y = xf * lax.rsqrt(jnp.mean(xf * xf, axis=-1, keepdims=True) + EPS)
    return (y * g.astype(jnp.float32)).astype(x.dtype)


def layernorm(x, g, b):
    xf = x.astype(jnp.float32)
    mu = jnp.mean(xf, axis=-1, keepdims=True)
    var = jnp.mean(jnp.square(xf - mu), axis=-1, keepdims=True)
    y = (xf - mu) * lax.rsqrt(var + EPS)
    return (y * g.astype(jnp.float32) + b.astype(jnp.float32)).astype(x.dtype)


def ada_params(cvec, w, b):
    m = jax.nn.silu(cvec) @ w + b
    m = m.reshape(m.shape[:-1] + (6, D_MODEL))
    if m.ndim == 3:
        m = m[:, None]
    return [m[..., i, :] for i in range(6)]


def modulate(x, g, shift, scale):
    return rmsnorm(x, g) * (1 + scale) + shift


def axial_rope_angles(n_tokens):
    rows = n_tokens // GRID_W
    r = jnp.repeat(jnp.arange(rows), GRID_W).astype(jnp.float32)
    col = jnp.tile(jnp.arange(GRID_W), rows).astype(jnp.float32)
    half = QK_ROPE // 2
    inv_freq = ROPE_THETA ** (-jnp.arange(0, half, 2, dtype=jnp.float32) / half)
    return r[:, None] * inv_freq[None, :], col[:, None] * inv_freq[None, :]


def rotate(x, ang):
    m = ang.shape[-1]
    cos = jnp.cos(ang)[:, None, :].astype(x.dtype)
    sin = jnp.sin(ang)[:, None, :].astype(x.dtype)
    x1, x2 = x[..., :m], x[..., m:]
    return jnp.concatenate([x1 * cos - x2 * sin, x2 * cos + x1 * sin], axis=-1)


def apply_axial_rope(x, angles):
    ang_r, ang_c = angles
    h = QK_ROPE // 2
    return jnp.concatenate([rotate(x[..., :h], ang_r), rotate(x[..., h:], ang_c)], axis=-1)


def blocked_attend(q, k, v):
    b, lq, h, d = q.shape
    nb = lq // Q_BLOCK
    scale = 1.0 / math.sqrt(d)
    qb = q.reshape(b, nb, Q_BLOCK, h, d).transpose(1, 0, 2, 3, 4)

    def one_block(qi):
        s = jnp.einsum('bqhd,bkhd->bhqk', qi, k).astype(jnp.float32) * scale
        p = jax.nn.softmax(s, axis=-1).astype(v.dtype)
        return jnp.einsum('bhqk,bkhd->bqhd', p, v)

    o = lax.map(one_block, qb)
    return o.transpose(1, 0, 2, 3, 4).reshape(b, lq, h * v.shape[-1])


def gla_scan(q, k, v, log_a, s0):
    b, l, h, dk = q.shape
    dv = v.shape[-1]
    nc = l // GLA_CHUNK

    def chunks(t):
        return t.astype(jnp.float32).reshape(b, nc, GLA_CHUNK, h, t.shape[-1]).transpose(1, 0, 3, 2, 4)

    causal = jnp.tril(jnp.ones((GLA_CHUNK, GLA_CHUNK), dtype=bool))[:, :, None]

    def step(state, inp):
        qc, kc, vc, gc = inp
        cum = jnp.cumsum(gc, axis=2)
        decay = jnp.exp(jnp.where(causal, cum[:, :, :, None, :] - cum[:, :, None, :, :], -jnp.inf))
        scores = jnp.einsum('bhid,bhjd,bhijd->bhij', qc, kc, decay)
        o = jnp.einsum('bhij,bhjv->bhiv', scores, vc) + jnp.einsum('bhid,bhdv->bhiv', qc * jnp.exp(cum), state)
        last = cum[:, :, -1, :]
        state = jnp.exp(last)[..., None] * state + jnp.einsum(
            'bhjd,bhjv->bhdv', kc * jnp.exp(last[:, :, None, :] - cum), vc)
        return state, o

    s_fin, o = lax.scan(step, s0.astype(jnp.float32), (chunks(q), chunks(k), chunks(v), chunks(log_a)))
    o = o.transpose(1, 0, 3, 2, 4).reshape(b, l, h, dv)
    return o.astype(v.dtype), s_fin


def gla_bidir(q, k, v, la_f, la_b, s0_f, s0_b):
    flip = lambda t: jnp.flip(t, axis=1)
    o_f, s_f = gla_scan(q, k, v, la_f, s0_f)
    o_b, s_b = gla_scan(flip(q), flip(k), flip(v), flip(la_b), s0_b)
    return o_f + flip(o_b), s_f, s_b


def split_in(proj):
    outs, start = [], 0
    for w in IN_SPLITS:
        outs.append(proj[..., start:start + w])
        start += w
    return outs


def mla_queries(cq, qa_norm, w_uq, q_gain, angles):
    b, l, _ = cq.shape
    q = (rmsnorm(cq, qa_norm) @ w_uq).reshape(b, l, MLA_HEADS, QK_HEAD)
    q = rmsnorm(q, q_gain)
    if angles is not None:
        q = jnp.concatenate([q[..., :QK_NOPE], apply_axial_rope(q[..., QK_NOPE:], angles)], axis=-1)
    return q


def mla_keys_values(ckv_n, krope, w_ukv, k_gain, angles):
    b, l, _ = ckv_n.shape
    kv = (ckv_n @ w_ukv).reshape(b, l, MLA_HEADS, QK_NOPE + V_HEAD)
    k_pe = jnp.broadcast_to(krope[:, :, None, :], (b, l, MLA_HEADS, QK_ROPE))
    k = rmsnorm(jnp.concatenate([kv[..., :QK_NOPE], k_pe], axis=-1), k_gain)
    if angles is not None:
        k = jnp.concatenate([k[..., :QK_NOPE], apply_axial_rope(k[..., QK_NOPE:], angles)], axis=-1)
    return k, kv[..., QK_NOPE:]


def even_mixer(h, p, j, angles, ctx):
    b, l, _ = h.shape
    cq, ckv, krope, gq, gk, gv, gg, ga = split_in(h @ p['w_in'][j])
    ckv_n = rmsnorm(ckv, p['kva_norm'][j])
    q = mla_queries(cq, p['qa_norm'][j], p['w_uq'][j], p['q_norm'][j], angles)
    k, v = mla_keys_values(ckv_n, krope, p['w_ukv'][j], p['k_norm'][j], angles)
    g_q = gq.reshape(b, l, GLA_HEADS, GLA_DK) * (GLA_DK ** -0.5)
    g_k = gk.reshape(b, l, GLA_HEADS, GLA_DK)
    g_v = gv.reshape(b, l, GLA_HEADS, GLA_DV)
    la = [jax.nn.log_sigmoid((ga[..., d * ALPHA_RANK:(d + 1) * ALPHA_RANK] @ p['w_alpha_up'][j, d]
                               + p['b_alpha'][j, d]).astype(jnp.float32)).reshape(b, l, GLA_HEADS, GLA_DK) / GLA_TAU
          for d in range(2)]
    if ctx is None:
        s0_f = jnp.zeros((b, GLA_HEADS, GLA_DK, GLA_DV), jnp.float32)
        s0_b = s0_f
    else:
        ckv_c, krope_c, s_c = ctx
        k_c, v_c = mla_keys_values(ckv_c, krope_c, p['w_ukv'][j], p['k_norm'][j], None)
        k = jnp.concatenate([k_c, k], axis=1)
        v = jnp.concatenate([v_c, v], axis=1)
        s0_f, s0_b = s_c[:, 0], s_c[:, 1]
    attn = blocked_attend(q, k, v)
    o, s_f, s_b = gla_bidir(g_q, g_k, g_v, la[0], la[1], s0_f, s0_b)
    gla_out = (rmsnorm(o, p['gla_norm'][j]) * jax.nn.silu(gg.reshape(b, l, GLA_HEADS, GLA_DV))).reshape(b, l, -1)
    out = jnp.concatenate([attn, gla_out], axis=-1) @ p['w_out'][j]
    if ctx is None:
        return out, (ckv_n, krope, jnp.stack([s_f, s_b], axis=1))
    return out, None


def conv_module(h, p, j):
    u = h @ p['conv_w_pw1'][j] + p['conv_b_pw1'][j]
    u = u[..., :D_MODEL] * jax.nn.sigmoid(u[..., D_MODEL:])
    u = lax.conv_general_dilated(u, p['conv_w_dw'][j][:, None, :], window_strides=(1,),
                                 padding=[(CONV_PAD, CONV_PAD)], dimension_numbers=('NWC', 'WIO', 'NWC'),
                                 feature_group_count=D_MODEL) + p['conv_b_dw'][j]
    u = jax.nn.silu(layernorm(u, p['conv_ln_g'][j], p['conv_ln_b'][j]))
    return u @ p['conv_w_pw2'][j] + p['conv_b_pw2'][j]


def swiglu(h, p, i):
    return (jax.nn.silu(h @ p['ffn_w_gate'][i]) * (h @ p['ffn_w_up'][i])) @ p['ffn_w_down'][i]


def run_trunk(x, cvec, p, angles, caches):
    ckv_l, kr_l, st_l = [], [], []
    for i in range(DEPTH):
        j = i // 2
        sh1, sc1, g1, sh2, sc2, g2 = ada_params(cvec, p['w_ada'][i], p['b_ada'][i])
        h = modulate(x, p['norm_mix'][i], sh1, sc1)
        if i % 2 == 0:
            ctx = None if caches is None else (caches[0][:, j], caches[1][:, j], caches[2][:, j])
            out, st = even_mixer(h, p, j, angles, ctx)
            if st is not None:
                ckv_l.append(st[0])
                kr_l.append(st[1])
                st_l.append(st[2])
        else:
            out = conv_module(h, p, j)
        x = x + g1 * out
        h = modulate(x, p['norm_ffn'][i], sh2, sc2)
        x = x + g2 * swiglu(h, p, i)
    return x, ckv_l, kr_l, st_l


def setup_inputs(seed: int = 0) -> dict:
    key = jax.random.key(seed)
    ks = list(jax.random.split(key, 40))

    def nrm(shape, scale=1.0):
        return jax.random.normal(ks.pop(), shape, jnp.float32) * scale

    def gain(shape):
        return 1.0 + nrm(shape, 0.1)

    D = D_MODEL
    return {
        'x_prompt': nrm((BATCH, SEQ, D)),
        'x_sample': nrm((DEC_BATCH, DEC_SEQ, D)),
        'cache_mla_ckv': nrm((DEC_BATCH, N_EVEN, PAST_LEN, KV_LORA)),
        'cache_mla_krope': nrm((DEC_BATCH, N_EVEN, PAST_LEN, QK_ROPE)),
        'state_gla': nrm((DEC_BATCH, N_EVEN, 2, GLA_HEADS, GLA_DK, GLA_DV)),
        'c': nrm((DEC_BATCH, D)),
        'c_ctx': nrm((D,)),
        'w_ada': nrm((DEPTH, D, 6 * D), 0.5 * D ** -0.5),
        'b_ada': nrm((DEPTH, 6 * D), 0.01),
        'norm_mix': gain((DEPTH, D)),
        'norm_ffn': gain((DEPTH, D)),
        'w_in': nrm((N_EVEN, D, IN_WIDTH), D ** -0.5),
        'qa_norm': gain((N_EVEN, Q_LORA)),
        'w_uq': nrm((N_EVEN, Q_LORA, MLA_HEADS * QK_HEAD), Q_LORA ** -0.5),
        'kva_norm': gain((N_EVEN, KV_LORA)),
        'w_ukv': nrm((N_EVEN, KV_LORA, MLA_HEADS * (QK_NOPE + V_HEAD)), KV_LORA ** -0.5),
        'q_norm': gain((N_EVEN, QK_HEAD)),
        'k_norm': gain((N_EVEN, QK_HEAD)),
        'w_alpha_up': nrm((N_EVEN, 2, ALPHA_RANK, GLA_HEADS * GLA_DK), ALPHA_RANK ** -0.5),
        'b_alpha': nrm((N_EVEN, 2, GLA_HEADS * GLA_DK), 0.1),
        'gla_norm': gain((N_EVEN, GLA_DV)),
        'w_out': nrm((N_EVEN, MIX_OUT, D), MIX_OUT ** -0.5),
        'conv_w_pw1': nrm((N_ODD, D, 2 * D), D ** -0.5),
        'conv_b_pw1': nrm((N_ODD, 2 * D), 0.01),
        'conv_w_dw': nrm((N_ODD, CONV_WIDTH, D), CONV_WIDTH ** -0.5),
        'conv_b_dw': nrm((N_ODD, D), 0.01),
        'conv_ln_g': gain((N_ODD, D)),
        'conv_ln_b': nrm((N_ODD, D), 0.01),
        'conv_w_pw2': nrm((N_ODD, D, D), D ** -0.5),
        'conv_b_pw2': nrm((N_ODD, D), 0.01),
        'ffn_w_gate': nrm((DEPTH, D, FFN_HIDDEN), D ** -0.5),
        'ffn_w_up': nrm((DEPTH, D, FFN_HIDDEN), D ** -0.5),
        'ffn_w_down': nrm((DEPTH, FFN_HIDDEN, D), FFN_HIDDEN ** -0.5),
    }


def reference(x_prompt, x_sample, cache_mla_ckv, cache_mla_krope, state_gla, c, c_ctx,
              w_ada, b_ada, norm_mix, norm_ffn, w_in, qa_norm, w_uq, kva_norm, w_ukv, q_norm, k_norm,
              w_alpha_up, b_alpha, gla_norm, w_out, conv_w_pw1, conv_b_pw1, conv_w_dw, conv_b_dw,
              conv_ln_g, conv_ln_b, conv_w_pw2, conv_b_pw2, ffn_w_gate, ffn_w_up, ffn_w_down):
    p = dict(w_ada=w_ada, b_ada=b_ada, norm_mix=norm_mix, norm_ffn=norm_ffn, w_in=w_in, qa_norm=qa_norm,
             w_uq=w_uq, kva_norm=kva_norm, w_ukv=w_ukv, q_norm=q_norm, k_norm=k_norm, w_alpha_up=w_alpha_up,
             b_alpha=b_alpha, gla_norm=gla_norm, w_out=w_out, conv_w_pw1=conv_w_pw1, conv_b_pw1=conv_b_pw1,
             conv_w_dw=conv_w_dw, conv_b_dw=conv_b_dw, conv_ln_g=conv_ln_g, conv_ln_b=conv_ln_b,
             conv_w_pw2=conv_w_pw2, conv_b_pw2=conv_b_pw2, ffn_w_gate=ffn_w_gate, ffn_w_up=ffn_w_up,
             ffn_w_down=ffn_w_down)
    y_prompt, ckv_l, kr_l, st_l = run_trunk(x_prompt, c_ctx, p, None, None)
    new_mla_ckv = jnp.stack(ckv_l, axis=1)
    new_mla_krope = jnp.stack(kr_l, axis=1)
    new_gla_state = jnp.stack(st_l, axis=1).astype(x_prompt.dtype)
    angles = axial_rope_angles(x_sample.shape[1])
    y_sample, _, _, _ = run_trunk(x_sample, c, p, angles, (cache_mla_ckv, cache_mla_krope, state_gla))
    return (y_prompt, y_sample, new_mla_ckv, new_mla_krope, new_gla_state)
```

```python
import math
import numpy as np
import concourse.bass as bass
import concourse.mybir as mybir
from concourse.bass_utils import run_bass_kernel_spmd
from contextlib import ExitStack

F32 = mybir.dt.float32
BF16 = mybir.dt.bfloat16
ALU = mybir.AluOpType
AF = mybir.ActivationFunctionType
AX = mybir.AxisListType

ENGS = ("sp", "act", "dve", "pool", "pe")
ARENA_WORDS = 52800


class Buf:
    __slots__ = ("name", "last_w", "readers", "dsem", "ap", "rng", "is_dram", "is_psum")

    def __init__(self, name, ap=None, is_dram=False, is_psum=False):
        self.name = name
        self.is_dram = is_dram
        self.is_psum = is_psum
        self.last_w = None
        self.readers = []
        self.dsem = None
        self.ap = ap
        self.rng = None


class Op:
    __slots__ = ("idx", "eng", "fn", "deps", "dma", "sem", "val", "signal", "name", "cost", "lat", "t0", "t1", "phase")


class Sched:
    def __init__(self, nc, arena_words):
        self.nc = nc
        self.ops = []
        self.arena_words = arena_words
        self.ghosts = []
        self.live = []
        self.dma_last = {}
        self.arena = None
        self.psum_bufs = []
        self.psum_rr = 0
        self.peak = 0

    def alloc(self, name, shape, dtype=F32):
        free = int(np.prod(shape[1:]))
        esz = 4 if dtype == F32 else 2
        words = (free * esz + 3) // 4
        self.live.sort(key=lambda t: t[0])
        pos = 0
        for (s, e, _) in self.live:
            if s - pos >= words:
                break
            pos = max(pos, e)
        if pos + words > self.arena_words:
            raise RuntimeError(f"SBUF arena OOM allocating {name} ({words} words); live="
                               f"{[(b.name, e - s) for s, e, b in self.live]}")
        self.peak = max(self.peak, pos + words)
        pp = getattr(self, "phase_peak", None)
        if pp is None:
            pp = self.phase_peak = {}
        ph = getattr(self, "phase", "")
        pp[ph] = max(pp.get(ph, 0), pos + words)
        ap = self.arena[:, pos:pos + words]
        if dtype != F32:
            ap = ap.bitcast(dtype)
            ap = ap[:, 0:free]
        if len(shape) > 2:
            names = " ".join(f"d{i}" for i in range(len(shape) - 1))
            kw = {f"d{i}": int(shape[i + 1]) for i in range(len(shape) - 1)}
            ap = ap.rearrange(f"p ({names}) -> p {names}", **kw)
        if shape[0] < 128:
            ap = ap[0:shape[0]]
        b = Buf(name, ap)
        b.rng = (pos, pos + words)
        inh = []
        for (s, e, ops) in self.ghosts:
            if s < pos + words and pos < e:
                inh.extend(ops)
        b.readers = list(dict.fromkeys(inh))
        self.live.append((pos, pos + words, b))
        return b

    def free(self, *bufs):
        for b in bufs:
            for i, (s, e, bb) in enumerate(self.live):
                if bb is b:
                    self.live.pop(i)
                    ops = list(b.readers)
                    if b.last_w is not None:
                        ops.append(b.last_w)
                    self.ghosts = [(gs, ge, go) for (gs, ge, go) in self.ghosts if not (gs >= s and ge <= e)]
                    if ops:
                        self.ghosts.append((s, e, ops))
                    break
            else:
                raise RuntimeError(f"free of non-live buf {b.name}")

    def op(self, eng, fn, reads=(), writes=(), dma=0, name="", cost=0.6, lat=0.0):
        o = Op()
        o.cost = cost
        o.lat = lat
        o.phase = getattr(self, "phase", "")
        o.idx = len(self.ops)
        o.eng = eng
        o.fn = fn
        o.dma = dma
        o.sem = None
        o.val = 0
        o.signal = False
        o.name = name
        deps = []
        for b in reads:
            if b.last_w is not None:
                deps.append(b.last_w)
            if b.is_psum:
                deps.extend(r_ for r_ in b.readers if r_.eng != eng)
        for b in writes:
            if b.last_w is not None:
                deps.append(b.last_w)
            deps.extend(b.readers)
        if dma:
            key = None
            for b in list(writes) + list(reads):
                if not b.is_dram:
                    key = b
                    break
            assert key is not None, name
            kname = key.name + ("_sw" if eng == "pool" else "")
            o.sem = kname
            prev = self.dma_last.get(kname)
            if prev is not None:
                deps.append(prev)
            self.dma_last[kname] = o
        seen = {}
        for d in deps:
            if d is o:
                continue
            seen[d.idx] = d
        o.deps = list(seen.values())
        for b in reads:
            b.readers.append(o)
        for b in writes:
            b.last_w = o
            b.readers = []
        self.ops.append(o)
        return o

    @staticmethod
    def _nowait(o, d):
        return o.eng == "pe" and d.eng == "pe" and not d.dma and not o.dma

    def reschedule(self, sync_lat=0.8):
        import heapq
        ops = self.ops
        n = len(ops)
        succ = [[] for _ in range(n)]
        ndep = [0] * n
        for o in ops:
            ndep[o.idx] = len(o.deps)
            for d in o.deps:
                succ[d.idx].append(o)
        bl = [0.0] * n
        for o in reversed(ops):
            m = 0.0
            for s_ in succ[o.idx]:
                extra = 0.0 if (s_.eng == o.eng and not o.dma) else sync_lat
                if bl[s_.idx] + extra > m:
                    m = bl[s_.idx] + extra
            bl[o.idx] = m + o.cost + o.lat
        ready_t = [0.0] * n
        pend = {e: [] for e in ENGS}
        avail = {e: [] for e in ENGS}
        for o in ops:
            if ndep[o.idx] == 0:
                heapq.heappush(pend[o.eng], (0.0, o.idx))
        t_free = {e: 0.0 for e in ENGS}
        order = []
        done = 0
        use_bl = PRIO_BL
        while done < n:
            best = None
            for e in ENGS:
                tf = t_free[e]
                pe_, av = pend[e], avail[e]
                while pe_ and pe_[0][0] <= tf:
                    rt, idx = heapq.heappop(pe_)
                    heapq.heappush(av, ((-bl[idx] if use_bl else idx), idx))
                if av:
                    st = tf
                    idx = av[0][1]
                elif pe_:
                    st = pe_[0][0]
                    idx = pe_[0][1]
                else:
                    continue
                if best is None or (st, idx) < (best[0], best[2]):
                    best = (st, e, idx)
            assert best is not None, "scheduler deadlock (cyclic deps?)"
            st, e, idx = best
            if avail[e] and avail[e][0][1] == idx:
                heapq.heappop(avail[e])
            else:
                heapq.heappop(pend[e])
            o = ops[idx]
            o.t0 = st
            t_free[e] = st + o.cost
            o.t1 = st + o.cost + o.lat
            order.append(o)
            done += 1
            for s_ in succ[idx]:
                extra = 0.0 if (s_.eng == o.eng and not o.dma) else sync_lat
                ready_t[s_.idx] = max(ready_t[s_.idx], o.t1 + extra)
                ndep[s_.idx] -= 1
                if ndep[s_.idx] == 0:
                    heapq.heappush(pend[s_.eng], (ready_t[s_.idx], s_.idx))
        self.ops = order
        for i, o in enumerate(order):
            o.idx = i
        self.sim_end = max(o.t1 for o in order)

    def finalize_plan(self, sem_alloc):
        for o in self.ops:
            for d in o.deps:
                if not self._nowait(o, d):
                    d.signal = True
        self.eng_sem = {}
        cnt = {e: 0 for e in ENGS}
        dcnt = {}
        dsems = {}
        for o in self.ops:
            if o.dma:
                key = o.sem
                if key not in dsems:
                    dsems[key] = sem_alloc(f"d_{key}")
                c = dcnt.get(key, 0) + 16 * o.dma
                dcnt[key] = c
                o.sem = dsems[key]
                o.val = c
            elif o.signal:
                if o.eng not in self.eng_sem:
                    self.eng_sem[o.eng] = sem_alloc(f"e_{o.eng}")
                cnt[o.eng] += 1
                o.sem = self.eng_sem[o.eng]
                o.val = cnt[o.eng]
        self.final_counts = cnt
        self.n_dsems = len(dsems)

    def run_engine(self, eng, e):
        waited = {}
        for o in self.ops:
            if o.eng != eng:
                continue
            need = {}
            for d in o.deps:
                if self._nowait(o, d):
                    continue
                k = id(d.sem)
                if k not in need or need[k][1] < d.val:
                    need[k] = (d.sem, d.val)
            for k, (sem, val) in need.items():
                if waited.get(k, 0) >= val:
                    continue
                e.wait_ge(sem, val)
                waited[k] = val
            r = o.fn(e)
            if o.dma:
                assert isinstance(r, (list, tuple)) and len(r) == o.dma, (o.name, r)
                for ins in r:
                    ins.then_inc(o.sem, 16)
            elif o.signal:
                assert r is not None, o.name
                r.then_inc(o.sem, 1)


D = 1024
NPT = 8
NST = 17
NSO = 16
NTILE_S = 32
NCTX = 2
NTOK = (NPT + NST) * 128
FFN = 2816
NJ = 22
NG = 11
EPS = 1e-6
SM_SCALE = 1.0 / math.sqrt(96.0)

RUN_SAMPLE = True
DEBUG = False
RESCHED = True
PRIO_BL = True
DVE_CONV_CHUNKS = ()


def pmaj(W):
    K, N = W.shape
    return np.ascontiguousarray(W.reshape(K // 128, 128, N).transpose(1, 0, 2))


def rope_tables(positions):
    half = 16
    inv_freq = (10000.0 ** (-np.arange(0, half, 2, dtype=np.float32) / half)).astype(np.float32)
    r = (positions // 64).astype(np.float32)
    c = (positions % 64).astype(np.float32)
    ang = np.concatenate([r[:, None] * inv_freq[None], c[:, None] * inv_freq[None]], axis=1).astype(np.float32)
    return np.cos(ang).astype(np.float32), np.sin(ang).astype(np.float32)


def const_tables():
    idx = np.arange(128)
    same = (idx[:, None] // 64) == (idx[None, :] // 64)
    c = -1.0 / 16.0
    trif = np.where(same & (idx[:, None] <= idx[None, :]), c, 0.0)
    trifs = np.where(same & (idx[:, None] > idx[None, :]), c, 0.0)
    trib = np.where(same & (idx[:, None] >= idx[None, :]), c, 0.0)
    tribs = np.where(same & (idx[:, None] < idx[None, :]), c, 0.0)
    maskf = np.where(same & (idx[:, None] <= idx[None, :]), 1.0, 0.0)
    maskb = np.where(same & (idx[:, None] >= idx[None, :]), 1.0, 0.0)
    ident = np.eye(128)
    return np.concatenate([ident, trif, trifs, trib, tribs, maskf, maskb], axis=1).astype(np.float32)


def prep_core(core, inp):
    b = core // 2
    rev = core % 2 == 1
    d = {}
    d["xp"] = np.ascontiguousarray(inp["x_prompt"][4 * core:4 * core + 4].reshape(1024, D))
    xs = inp["x_sample"][b]
    d["xs"] = np.ascontiguousarray(xs[::-1] if rev else xs)
    d["cckv"] = np.ascontiguousarray(inp["cache_mla_ckv"][b, 0])
    d["ckr"] = np.ascontiguousarray(inp["cache_mla_krope"][b, 0])
    st = inp["state_gla"][b, 0]
    d["st0"] = np.ascontiguousarray(st[::-1] if rev else st)
    cc = np.stack([inp["c_ctx"], inp["c"][b]], axis=1)
    d["cT"] = np.ascontiguousarray(cc.reshape(8, 128, 2).transpose(1, 0, 2))
    pos = np.arange(4096)
    if rev:
        pos = pos[::-1]
    cs, sn = rope_tables(pos)
    rt = np.stack([cs, sn], axis=1)
    d["rope"] = np.ascontiguousarray(rt.reshape(32, 128, 2, 16).transpose(1, 0, 2, 3))
    wal = np.zeros((2, 33, 2, 256), np.float32)
    for g in range(2):
        for ld in range(2):
            od = (1 - ld) if (g == 1 and rev) else ld
            wal[g, od * 16:(od + 1) * 16, ld, :] = inp["w_alpha_up"][0, od]
            wal[g, 32, ld, :] = inp["b_alpha"][0, od]
    d["wal"] = wal
    cw = inp["conv_w_dw"][0]
    cws = cw[::-1] if rev else cw
    cwt = np.stack([cw, cws], axis=0)
    d["cdw"] = np.ascontiguousarray(cwt.reshape(2, 31, 8, 128).transpose(3, 0, 2, 1))
    return d


def prep_shared(inp):
    d = {}
    w_in = inp["w_in"][0]
    o = np.cumsum([0, 384, 256, 32, 256, 256, 512, 512, 32])
    cq, ckv, kr, gq, gk, gv, gg, ga = [w_in[:, o[i]:o[i + 1]] for i in range(8)]
    d["w_mla"] = pmaj(np.concatenate([cq, ckv, kr], axis=1))
    d["w_gfm"] = pmaj(np.concatenate([gq, gk, gg, ga], axis=1))
    d["w_gtm"] = pmaj(np.concatenate([gk, gv], axis=1))
    d["w_uq"] = pmaj(inp["w_uq"][0])
    wukv = inp["w_ukv"][0].reshape(256, 8, 128)
    d["w_ukv"] = pmaj(np.concatenate([wukv[:, :, :64].reshape(256, 512), wukv[:, :, 64:].reshape(256, 512)], axis=1))
    d["w_out"] = pmaj(inp["w_out"][0])
    d["pw1"] = pmaj(inp["conv_w_pw1"][0])
    d["pw2"] = pmaj(inp["conv_w_pw2"][0])
    for l in range(2):
        d[f"wg{l}"] = np.ascontiguousarray(inp["ffn_w_gate"][l].reshape(8, 128, NG, 256).transpose(2, 1, 0, 3))
        d[f"wu{l}"] = np.ascontiguousarray(inp["ffn_w_up"][l].reshape(8, 128, NG, 256).transpose(2, 1, 0, 3))
        d[f"wd{l}"] = np.ascontiguousarray(inp["ffn_w_down"][l].reshape(NG, 2, 128, D).transpose(0, 2, 1, 3))
        d[f"wada{l}"] = np.ascontiguousarray(inp["w_ada"][l].reshape(8, 128, 12, 512).transpose(2, 1, 0, 3))
    d["bada"] = np.ascontiguousarray(inp["b_ada"])
    d["v_nm"] = np.ascontiguousarray(inp["norm_mix"])
    d["v_nf"] = np.ascontiguousarray(inp["norm_ffn"])
    qg = np.tile(inp["q_norm"][0], 8)
    kg = inp["k_norm"][0]
    vec = np.zeros((1, 8192), np.float32)
    def put(off, v):
        vec[0, off:off + v.size] = v.reshape(-1)
    put(0, inp["qa_norm"][0])
    put(384, inp["kva_norm"][0])
    put(640, qg)
    put(1408, np.tile(kg[:64], 8))
    put(1920, kg[64:])
    put(2048, inp["conv_b_pw2"][0])
    d["vec"] = vec
    fm = np.zeros((128, 64), np.float32)
    fm[:, 0] = inp["gla_norm"][0]
    fm[:, 1:9] = inp["conv_b_pw1"][0][:1024].reshape(8, 128).T
    fm[:, 9:17] = inp["conv_b_pw1"][0][1024:].reshape(8, 128).T
    fm[:, 17:25] = inp["conv_b_dw"][0].reshape(8, 128).T
    fm[:, 25:33] = inp["conv_ln_g"][0].reshape(8, 128).T
    fm[:, 33:41] = inp["conv_ln_b"][0].reshape(8, 128).T
    d["fm"] = fm
    d["cst"] = const_tables()
    return d


class PB:
    def __init__(self, nc, S, I, O):
        self.nc, self.S, self.I, self.O = nc, S, I, O
        self.outstores = []
        self.psA = S.psum_bufs[0:6]
        self.psB = S.psum_bufs[6:8]
        self.rrA = 0
        self.cvec = None

    def T(self, name, shape, dt=F32):
        return self.S.alloc(name, shape, dt)

    @staticmethod
    def _n(ap):
        return int(np.prod(ap.shape[1:]))

    def _ec(self, eng, ap):
        n = self._n(ap)
        if eng == "dve":
            return 0.1 + n / 1200.0
        if eng == "act":
            return 0.19 + n / 1900.0
        return 0.3 + n / 250.0

    def free(self, *b):
        self.S.free(*b)

    def ps(self):
        b = self.psA[self.rrA % len(self.psA)]
        self.rrA += 1
        return b

    def TT(self, eng, out, in0, in1, op, r, w):
        def fn(e):
            return e.tensor_tensor(out=out, in0=in0, in1=in1, op=op)
        return self.S.op(eng, fn, r, w, name="TT", cost=self._ec(eng, out))

    def STT(self, eng, out, in0, scalar, in1, op0, op1, r, w):
        def fn(e):
            return e.scalar_tensor_tensor(out=out, in0=in0, scalar=scalar, in1=in1, op0=op0, op1=op1)
        return self.S.op(eng, fn, r, w, name="STT", cost=self._ec(eng, out))

    def TS(self, eng, out, in0, s1, s2, op0, op1, r, w):
        def fn(e):
            if s2 is None:
                return e.tensor_scalar(out=out, in0=in0, scalar1=s1, scalar2=None, op0=op0)
            return e.tensor_scalar(out=out, in0=in0, scalar1=s1, scalar2=s2, op0=op0, op1=op1)
        return self.S.op(eng, fn, r, w, name="TS", cost=self._ec(eng, out))

    def CP(self, eng, out, in_, r, w):
        def fn(e):
            if eng == "act":
                return e.copy(out=out, in_=in_)
            return e.tensor_copy(out=out, in_=in_)
        return self.S.op(eng, fn, r, w, name="CP", cost=self._ec(eng, out))

    def ACT(self, out, in_, func, r, w, bias=None, scale=1.0, accum=None):
        def fn(e):
            kw = {}
            if bias is not None:
                kw["bias"] = bias
            if accum is not None:
                kw["accum_out"] = accum
            return e.activation(out=out, in_=in_, func=func, scale=scale, **kw)
        return self.S.op("act", fn, r, w, name="ACT", cost=self._ec("act", out))

    def MEMSET(self, eng, ap, val, w):
        def fn(e):
            return e.memset(ap, val)
        return self.S.op(eng, fn, (), w, name="MEMSET", cost=self._ec(eng, ap))

    def RED(self, out, in_, r, w):
        def fn(e):
            return e.tensor_reduce(out=out, in_=in_, axis=AX.X, op=ALU.add)
        return self.S.op("dve", fn, r, w, name="RED", cost=self._ec("dve", in_))

    def RECIP(self, out, in_, r, w):
        def fn(e):
            return e.reciprocal(out=out, in_=in_)
        return self.S.op("dve", fn, r, w, name="RECIP", cost=self._ec("dve", out) * 2)

    def load(self, dst, src_ap, eng="sp", r=(), dst_ap=None, name="ld"):
        da = dst.ap if dst_ap is None else dst_ap
        def fn(e):
            return [e.dma_start(out=da, in_=src_ap)]
        nb = self._n(da) * 128 * (4 if da.dtype == F32 else 2)
        return self.S.op(eng, fn, reads=list(r), writes=[dst], dma=1, name=name, cost=(0.15 if eng == "sp" else 1.0), lat=2.0 + nb / 2.0e5)

    def store(self, dst_ap, src, eng="sp", w=(), src_ap=None, final=False, name="st"):
        sa = src.ap if src_ap is None else src_ap
        def fn(e):
            return [e.dma_start(out=dst_ap, in_=sa)]
        nb = self._n(sa) * 128 * (4 if sa.dtype == F32 else 2)
        o = self.S.op(eng, fn, reads=[src], writes=list(w), dma=1, name=name, cost=(0.15 if eng == "sp" else 1.0), lat=2.0 + nb / 2.0e5)
        if final:
            self.outstores.append(o)
        return o

    def mm(self, psbs, groups, r, name="mm"):
        if not isinstance(psbs, (list, tuple)):
            psbs = [psbs]
        groups = [g if len(g) == 4 else (g[0], g[1], True, True) for g in groups]
        def fn(e):
            last = None
            for (pa, pairs, st, sp_) in groups:
                n = len(pairs)
                for i, (l, rh) in enumerate(pairs):
                    last = e.matmul(pa, lhsT=l, rhs=rh, start=(st and i == 0), stop=(sp_ and i == n - 1))
            return last
        c = 0.0
        for (pa, pairs, st, sp_) in groups:
            for (l, rh) in pairs:
                c += max(64, self._n(rh)) / 2200.0 + 0.02
        return self.S.op("pe", fn, r, list(psbs), name=name, cost=c, lat=0.1)

    def tr(self, psb, items, r, ident, name="tr"):
        items = list(items)
        def fn(e):
            last = None
            for (pa, ia) in items:
                last = e.transpose(out=pa, in_=ia, identity=ident)
            return last
        return self.S.op("pe", fn, r, [psb], name=name, cost=0.1 * len(items), lat=0.1)

    def rstd(self, ss_buf, ss_ap, out_buf, out_ap, inv_n, tmp_buf, tmp_ap):
        npart = ss_ap.shape[0]
        if (not ss_buf.is_psum) and self._n(ss_ap) <= 8:
            self.TS("pool", tmp_ap, ss_ap, inv_n, EPS, ALU.mult, ALU.add, [ss_buf], [tmp_buf])
            self.TT("pool", out_ap, tmp_ap, self.cvec.ap[0:npart, 3:4].to_broadcast([npart, self._n(ss_ap)]), ALU.pow,
                    [tmp_buf, self.cvec], [out_buf])
            return
        self.ACT(tmp_ap, ss_ap, AF.Ln, [ss_buf, self.cvec], [tmp_buf], bias=self.cvec.ap[0:npart, 0:1], scale=inv_n)
        self.ACT(out_ap, tmp_ap, AF.Exp, [tmp_buf], [out_buf], scale=-0.5)


def build_program(nc, S, I, O):
    P = PB(nc, S, I, O)

    def dram(name, shape, dt=F32):
        ap = nc.dram_tensor(name, list(shape), dt, kind=("ExternalOutput" if DEBUG else "Internal")).ap()
        return Buf(name, ap, is_dram=True)
    MODa = [dram(f"MODa{l}", [2, 2 * D]) for l in range(2)]
    MODb = [dram(f"MODb{l}", [2, 4 * D]) for l in range(2)]
    X2 = dram("X2", [NTOK, D])
    UT = dram("UT", [D, NTOK])
    ATT = dram("ATT", [NTOK, 512], BF16)
    GLT = dram("GLT", [4, 128, NTOK], BF16)
    HTS = dram("HTS", [NPT + NTILE_S, 128, 8, 128], BF16)
    WG = [dram(f"WGs{l}", [NG, 128, 8, 256], BF16) for l in range(2)]
    WU = [dram(f"WUs{l}", [NG, 128, 8, 256], BF16) for l in range(2)]
    WD = [dram(f"WDs{l}", [NG, 128, 2, 1024], BF16) for l in range(2)]

    def precast_ffn():
        stg = [P.T(f"stg{i}", [128, 2048], BF16) for i in range(4)]
        work = []
        for l in range(2):
            for gi in range(NG):
                work.append((I[f"wg{l}"][gi].rearrange("p a b -> p (a b)"), WG[l], WG[l].ap[gi].rearrange("p a b -> p (a b)")))
                work.append((I[f"wu{l}"][gi].rearrange("p a b -> p (a b)"), WU[l], WU[l].ap[gi].rearrange("p a b -> p (a b)")))
                work.append((I[f"wd{l}"][gi].rearrange("p a b -> p (a b)"), WD[l], WD[l].ap[gi].rearrange("p a b -> p (a b)")))
        LEAD = 3
        n = len(work)
        for i in range(n + LEAD):
            if i < n:
                P.load(stg[i % 4], work[i][0], eng="pool", name="precast_ld")
            k = i - LEAD
            if k >= 0:
                P.store(work[k][2], stg[k % 4], eng="pool", w=[work[k][1]], name="precast_st")
        return stg

    cst = P.T("cst", [128, 7 * 128])
    P.load(cst, I["cst"])
    ident_f = cst.ap[:, 0:128]
    TRI = {("f", 0): cst.ap[:, 128:256], ("f", 1): cst.ap[:, 256:384], ("b", 0): cst.ap[:, 384:512], ("b", 1): cst.ap[:, 512:640]}
    MASK = {"f": cst.ap[:, 640:768], "b": cst.ap[:, 768:896]}
    identb = P.T("identb", [128, 128], BF16)
    P.CP("dve", identb.ap, ident_f, [cst], [identb])
    onesb = P.T("onesb", [128, 128], BF16)
    P.MEMSET("pool", onesb.ap, 1.0, [onesb])
    cvec = P.T("cvec", [128, 4])
    P.cvec = cvec
    P.MEMSET("pool", cvec.ap[:, 0:1], EPS, [cvec])
    P.MEMSET("pool", cvec.ap[:, 2:3], 1.0, [cvec])
    P.MEMSET("pool", cvec.ap[:, 3:4], -0.5, [cvec])
    fm = P.T("fm", [128, 64])
    P.load(fm, I["fm"])
    rope = P.T("rope", [128, 32, 2, 16])
    P.load(rope, I["rope"])

    def ada_layer(l, part):
        pcs = range(0, 4) if part == 0 else range(4, 12)
        ncol = len(pcs) * 512
        c0 = pcs[0] * 512
        cT = P.T("cT", [128, 8, 2])
        P.load(cT, I["cT"])
        scb = P.T("scb", [128, 8, 2], BF16)
        P.ACT(scb.ap, cT.ap, AF.Silu, [cT], [scb])
        bad = P.T("bad", [2, ncol])
        P.load(bad, I["bada"][l:l + 1, c0:c0 + ncol].partition_broadcast(2))
        mo = P.T("mo", [2, ncol])
        wp = [P.T(f"wada{i}", [128, 8, 512], BF16) for i in range(2)]
        for i, pc in enumerate(pcs):
            w = wp[i % 2]
            P.load(w, I[f"wada{l}"][pc], eng="pool")
            psb = P.ps()
            P.mm(psb, [(psb.ap[0:2, :], [(scb.ap[:, k, :], w.ap[:, k, :]) for k in range(8)])], r=[scb, w])
            P.TT("dve", mo.ap[:, i * 512:(i + 1) * 512], psb.ap[0:2, :], bad.ap[:, i * 512:(i + 1) * 512], ALU.add, [psb, bad], [mo])
        dstb = (MODa if part == 0 else MODb)[l]
        P.store(dstb.ap, mo, w=[dstb])
        P.free(cT, scb, bad, mo, *wp)

    def load_mod(l, g, which, name):
        out = {}
        for k in which:
            t = P.T(f"mod{name}{k}", [128, D])
            if k < 2:
                src, kk = MODa[l], k
            else:
                src, kk = MODb[l], k - 2
            P.load(t, src.ap[g:g + 1, kk * D:(kk + 1) * D].partition_broadcast(128), r=[src])
            out[k] = t
        return out

    def make_A(scbuf, gain_row_ap, name):
        gt = P.T(name + "_g", [128, D])
        P.load(gt, gain_row_ap.partition_broadcast(128))
        P.STT("dve", scbuf.ap, scbuf.ap, 1.0, gt.ap, ALU.add, ALU.mult, [scbuf, gt], [scbuf])
        P.free(gt)

    def mk_wk(pfx):
        return dict(ss=P.T(pfx + "ss", [128, 1]), t1=P.T(pfx + "t1", [128, 1]), junk=P.T(pfx + "junk", [128, D], BF16),
                    tmp=P.T(pfx + "tmp", [128, D]), hb=P.T(pfx + "hb", [128, D], BF16))

    def free_wk(wk):
        P.free(*wk.values())

    def tm_norm_mod(xt, A, Bm, hT_buf, hT_ap, wk):
        ss, t1, junk, tmp, hb = wk["ss"], wk["t1"], wk["junk"], wk["tmp"], wk["hb"]
        P.ACT(junk.ap, xt.ap, AF.Square, [xt], [junk, ss], accum=ss.ap)
        P.rstd(ss, ss.ap, ss, ss.ap, 1.0 / D, t1, t1.ap)
        P.STT("dve", tmp.ap, xt.ap, ss.ap, A.ap, ALU.mult, ALU.mult, [xt, ss, A], [tmp])
        P.TT("dve", hb.ap, tmp.ap, Bm.ap, ALU.add, [tmp, Bm], [hb])
        psb = P.ps()
        pv = psb.ap.bitcast(BF16).rearrange("p (c t) -> p c t", c=8)
        P.tr(psb, [(pv[:, c, :], hb.ap[:, c * 128:(c + 1) * 128]) for c in range(8)], r=[hb, identb], ident=identb.ap)
        P.CP("act", hT_ap, pv, [psb], [hT_buf])

    def rope_apply(src_buf, src_ap, dst_buf, dst_ap, tile_idx, nh, tb1, tb2):
        cs = rope.ap[:, tile_idx, 0, :].rearrange("p (a f) -> p a f", a=2)
        sn = rope.ap[:, tile_idx, 1, :].rearrange("p (a f) -> p a f", a=2)
        s5 = src_ap.rearrange("p h (a b f) -> p h a b f", a=2, b=2)
        d5 = dst_ap.rearrange("p h (a b f) -> p h a b f", a=2, b=2)
        t5 = tb1.ap.rearrange("p (h a b f) -> p h a b f", h=nh, a=2, b=2)
        x5 = tb2.ap.rearrange("p (h a b f) -> p h a b f", h=nh, a=2, b=2)
        csb = cs.unsqueeze(1).to_broadcast([128, nh, 2, 8])
        snb = sn.unsqueeze(1).to_broadcast([128, nh, 2, 8])
        P.TT("dve", t5[:, :, :, 0, :], s5[:, :, :, 1, :], snb, ALU.mult, [src_buf, rope], [tb1])
        P.TT("dve", t5[:, :, :, 1, :], s5[:, :, :, 0, :], snb, ALU.mult, [src_buf, rope, tb1], [tb1])
        P.TT("dve", x5[:, :, :, 0, :], s5[:, :, :, 0, :], csb, ALU.mult, [src_buf, rope], [tb2])
        P.TT("dve", x5[:, :, :, 1, :], s5[:, :, :, 1, :], csb, ALU.mult, [src_buf, rope, tb2], [tb2])
        P.TT("dve", d5[:, :, :, 0, :], x5[:, :, :, 0, :], t5[:, :, :, 0, :], ALU.subtract, [tb1, tb2], [dst_buf])
        P.TT("dve", d5[:, :, :, 1, :], x5[:, :, :, 1, :], t5[:, :, :, 1, :], ALU.add, [tb1, tb2, dst_buf], [dst_buf])

    jobs = []
    for s in range(4):
        jobs.append(dict(name=f"p{s}", g=0, xsrc=I["xp"], ext=[2 * s, 2 * s + 1], other=[], ctx=False, rope=False,
                         row0=2 * s * 128, seq=s))
    if RUN_SAMPLE:
        jobs.append(dict(name="s", g=1, xsrc=I["xs"], ext=list(range(NST)), other=list(range(NST, NTILE_S)), ctx=True, rope=True,
                         row0=NPT * 128, seq=None))

    S.phase = 'ada0'
    ada_layer(0, 0)
    ada_layer(0, 1)

    def mla_pass():
        w_mla = P.T("w_mla", [128, 8, 672], BF16)
        P.load(w_mla, I["w_mla"], eng="pool")
        w_uq = P.T("w_uq", [128, 3, 768], BF16)
        P.load(w_uq, I["w_uq"], eng="pool")
        w_ukv = P.T("w_ukv", [128, 2, 1024], BF16)
        P.load(w_ukv, I["w_ukv"], eng="pool")
        vecm = P.T("vecm", [128, 1952])
        P.load(vecm, I["vec"][0:1, 0:1952].partition_broadcast(128))
        qa_bc = vecm.ap[:, 0:384]
        kva_bc = vecm.ap[:, 384:640]
        qg_bc = vecm.ap[:, 640:1408]
        kgn_bc = vecm.ap[:, 1408:1920]
        kgr_bc = vecm.ap[:, 1920:1952]
        NKMAX = (NCTX + NTILE_S) if RUN_SAMPLE else 2
        KT = P.T("KT", [96, 8, NKMAX * 128], BF16)
        VA = P.T("VA", [128, NKMAX, 8, 65], BF16)
        P.MEMSET("pool", VA.ap[:, :, :, 64:65], 1.0, [VA])
        NQMAX = NST if RUN_SAMPLE else 2
        cqT = P.T("cqT", [128, 3, NQMAX * 128], BF16)
        stg_keep = []

        def build_keys(job):
            g = job["g"]
            m = load_mod(0, g, [0, 1], "")
            make_A(m[1], I["v_nm"][0:1, :], "mod")
            A_, B_ = m[1], m[0]
            wks = [mk_wk(f"m{i_}_") for i_ in range(2)]
            xts_ = [P.T(f"m_x{i_}", [128, D]) for i_ in range(2)]
            hTs_ = [P.T(f"m_hT{i_}", [128, 8, 128], BF16) for i_ in range(2)]
            kcnt = [0]

            class BS_:
                pass
            bsets = []
            allb = [A_, B_, *xts_, *hTs_]
            for i_ in range(2):
                b_ = BS_()
                b_.ssq, b_.ssk, b_.ssr, b_.t1 = (P.T(f"m{i_}_{n}", [128, 1]) for n in ("ssq", "ssk", "ssr", "t1b"))
                b_.ssn, b_.rs, b_.t8 = (P.T(f"m{i_}_{n}", [128, 8]) for n in ("ssn", "rs", "t8"))
                b_.cqn = P.T(f"m{i_}_cqn", [128, 384], BF16)
                b_.ckvn = P.T(f"m{i_}_ckvn", [128, 256])
                b_.ckvb = P.T(f"m{i_}_ckvb", [128, 256], BF16)
                b_.ckT = P.T(f"m{i_}_ckT", [128, 2, 128], BF16)
                b_.krr = P.T(f"m{i_}_krr", [128, 32])
                b_.krg = P.T(f"m{i_}_krg", [128, 32])
                b_.krt1 = P.T(f"m{i_}_krt1", [128, 32])
                b_.krt2 = P.T(f"m{i_}_krt2", [128, 32])
                b_.kro = P.T(f"m{i_}_kro", [128, 32])
                b_.krj = P.T(f"m{i_}_krj", [128, 32])
                b_.ktm = P.T(f"m{i_}_ktm", [128, 8, 96], BF16)
                b_.sq5 = P.T(f"m{i_}_sq5", [128, 512])
                b_.kn2 = P.T(f"m{i_}_kn2", [128, 512])
                allb += [b_.ssq, b_.ssk, b_.ssr, b_.t1, b_.ssn, b_.rs, b_.t8, b_.cqn, b_.ckvn, b_.ckvb, b_.ckT, b_.krr, b_.krg, b_.krt1,
                         b_.krt2, b_.kro, b_.krj, b_.ktm, b_.sq5, b_.kn2]
                bsets.append(b_)
            cur = [bsets[0], wks[0]]

            def kv_from_latent(ktile, ck_buf, ck_ap, kr_buf, kr_ap, rope_tile):
                b_ = cur[0]
                ssr, ssn, rs, t8, ckvb, ckT, krg, krt1, krt2, kro, krj, ktm, sq5, kn2 = (b_.ssr, b_.ssn, b_.rs, b_.t8, b_.ckvb, b_.ckT, b_.krg,
                                                                                      b_.krt1, b_.krt2, b_.kro, b_.krj, b_.ktm, b_.sq5, b_.kn2)
                P.CP("dve", ckvb.ap, ck_ap, [ck_buf], [ckvb])
                psb = P.ps()
                pv = psb.ap.bitcast(BF16)[:, 0:256].rearrange("p (c t) -> p c t", c=2)
                P.tr(psb, [(pv[:, c, :], ckvb.ap[:, c * 128:(c + 1) * 128]) for c in range(2)], r=[ckvb, identb], ident=identb.ap)
                P.CP("act", ckT.ap, pv, [psb], [ckT])
                psk, psv = P.ps(), P.ps()
                P.mm(psk, [(psk.ap, [(ckT.ap[:, c, :], w_ukv.ap[:, c, 0:512]) for c in range(2)])], r=[ckT, w_ukv])
                P.mm(psv, [(psv.ap, [(ckT.ap[:, c, :], w_ukv.ap[:, c, 512:1024]) for c in range(2)])], r=[ckT, w_ukv])
                P.CP("act", VA.ap[:, ktile, :, 0:64], psv.ap.rearrange("p (h d) -> p h d", h=8), [psv], [VA])
                P.ACT(sq5.ap, psk.ap, AF.Square, [psk], [sq5])
                P.RED(ssn.ap, sq5.ap.rearrange("p (h d) -> p h d", h=8), [sq5], [ssn])
                P.ACT(krj.ap, kr_ap, AF.Square, [kr_buf], [krj, ssr], accum=ssr.ap)
                P.TS("dve", ssn.ap, ssn.ap, ssr.ap, None, ALU.add, None, [ssn, ssr], [ssn])
                P.rstd(ssn, ssn.ap, rs, rs.ap, 1.0 / 96.0, t8, t8.ap)
                P.TT("dve", kn2.ap, psk.ap, kgn_bc, ALU.mult, [psk, vecm], [kn2])
                P.TT("dve", ktm.ap[:, :, 0:64], kn2.ap.rearrange("p (h d) -> p h d", h=8),
                     rs.ap.unsqueeze(2).to_broadcast([128, 8, 64]), ALU.mult, [kn2, rs], [ktm])
                P.TT("dve", krg.ap, kr_ap, kgr_bc, ALU.mult, [kr_buf, vecm], [krg])
                if rope_tile is not None:
                    rope_apply(krg, krg.ap.unsqueeze(1), kro, kro.ap.unsqueeze(1), rope_tile, 1, krt1, krt2)
                    ksrc = kro
                else:
                    ksrc = krg
                P.TT("dve", ktm.ap[:, :, 64:96], ksrc.ap.unsqueeze(1).to_broadcast([128, 8, 32]),
                     rs.ap.unsqueeze(2).to_broadcast([128, 8, 32]), ALU.mult, [ksrc, rs], [ktm])
                pst = P.ps()
                ptv = pst.ap.bitcast(BF16).rearrange("p (h t) -> p h t", h=8)
                P.tr(pst, [(ptv[0:96, h, :], ktm.ap[:, h, :]) for h in range(8)], r=[ktm, identb], ident=identb.ap)
                P.CP("act", KT.ap[:, :, ktile * 128:(ktile + 1) * 128], ptv[0:96], [pst], [KT])

            def key_tile(t, kidx, full):
                xt = xts_[kcnt[0] % 2]
                hT = hTs_[kcnt[0] % 2]
                cur[0] = bsets[kcnt[0] % 2]
                cur[1] = wks[kcnt[0] % 2]
                kcnt[0] += 1
                b_ = cur[0]
                wk = cur[1]
                ssq, ssk, t1, cqn, ckvn, krr = b_.ssq, b_.ssk, b_.t1, b_.cqn, b_.ckvn, b_.krr
                P.load(xt, job["xsrc"][t * 128:(t + 1) * 128, :])
                tm_norm_mod(xt, A_, B_, hT, hT.ap, wk)
                P.store(HTS.ap[t if g == 0 else NPT + t], hT, eng="pool", w=[HTS])
                psB_ = P.ps()
                P.mm(psB_, [(psB_.ap[:, 0:288], [(hT.ap[:, c, :], w_mla.ap[:, c, 384:672]) for c in range(8)])], r=[hT, w_mla])
                if full:
                    psA_ = P.ps()
                    P.mm(psA_, [(psA_.ap[:, 0:384], [(hT.ap[:, c, :], w_mla.ap[:, c, 0:384]) for c in range(8)])], r=[hT, w_mla])
                    P.ACT(wk["junk"].ap[:, 0:384], psA_.ap[:, 0:384], AF.Square, [psA_], [wk["junk"], ssq], accum=ssq.ap)
                    P.rstd(ssq, ssq.ap, ssq, ssq.ap, 1.0 / 384.0, t1, t1.ap)
                    P.STT("dve", cqn.ap, psA_.ap[:, 0:384], ssq.ap, qa_bc, ALU.mult, ALU.mult, [psA_, ssq, vecm], [cqn])
                    pst = P.ps()
                    pv = pst.ap.bitcast(BF16)[:, 0:384].rearrange("p (c t) -> p c t", c=3)
                    P.tr(pst, [(pv[:, c, :], cqn.ap[:, c * 128:(c + 1) * 128]) for c in range(3)], r=[cqn, identb], ident=identb.ap)
                    qi = job["ext"].index(t)
                    P.CP("act", cqT.ap[:, :, qi * 128:(qi + 1) * 128], pv, [pst], [cqT])
                P.ACT(wk["junk"].ap[:, 0:256], psB_.ap[:, 0:256], AF.Square, [psB_], [wk["junk"], ssk], accum=ssk.ap)
                P.rstd(ssk, ssk.ap, ssk, ssk.ap, 1.0 / 256.0, t1, t1.ap)
                P.STT("dve", ckvn.ap, psB_.ap[:, 0:256], ssk.ap, kva_bc, ALU.mult, ALU.mult, [psB_, ssk, vecm], [ckvn])
                P.CP("act", krr.ap, psB_.ap[:, 256:288], [psB_], [krr])
                if g == 0:
                    row = t * 128
                    P.store(O["ckv_o"][row:row + 128, :], ckvn, final=True)
                    P.store(O["kr_o"][row:row + 128, :], krr, final=True)
                kv_from_latent(kidx, ckvn, ckvn.ap, krr, krr.ap, t if job["rope"] else None)

            kidx = 0
            if job["ctx"]:
                for ct in range(NCTX):
                    cl = P.T("c_lat", [128, 256])
                    ck = P.T("c_kr", [128, 32])
                    P.load(cl, I["cckv"][ct * 128:(ct + 1) * 128, :])
                    P.load(ck, I["ckr"][ct * 128:(ct + 1) * 128, :])
                    cur[0] = bsets[ct % 2]
                    kv_from_latent(kidx, cl, cl.ap, ck, ck.ap, None)
                    P.free(cl, ck)
                    kidx += 1
            for t in job["ext"]:
                key_tile(t, kidx, True)
                kidx += 1
            for t in job["other"]:
                key_tile(t, kidx, False)
                kidx += 1
            for wk_ in wks:
                free_wk(wk_)
            P.free(*allb)
            return kidx

        def pv_op(PT, kc, ov, h, nt, nk, pso):
            def fn(e):
                last = None
                for j in range(nt):
                    last = e.matmul(ov[:, j, :], lhsT=PT.ap[:, j * 128:(j + 1) * 128], rhs=VA.ap[:, kc, h, :],
                                    start=(kc == 0 and j == 0), stop=(kc == nk - 1), skip_group_check=True)
                return last
            P.S.op("pe", fn, [PT, VA], [pso], name="pv", cost=0.07 * nt, lat=0.1)

        def attention(job, nk):
            ext = job["ext"]
            nq = len(ext)
            qsb = P.T("a_qsb", [128, 768])
            qsq = P.T("a_qsq", [128, 768])
            qn = P.T("a_qn", [128, 8, 96])
            qrp = P.T("a_qrp", [128, 8, 32])
            qrt1 = P.T("a_qrt1", [128, 256])
            qrt2 = P.T("a_qrt2", [128, 256])
            qbf = P.T("a_qbf", [128, 8, 96], BF16)
            ssq8 = P.T("a_ssq8", [128, 8])
            rq8 = P.T("a_rq8", [128, 8])
            t8 = P.T("a_t8", [128, 8])
            qTs = [P.T(f"a_qT{i}", [96, 8, 512], BF16) for i in range(2)]
            PTs = [P.T(f"a_PT{i}", [128, 512], BF16) for i in range(3)]
            rc = P.T("a_rc", [128, 4])
            aos = [P.T(f"a_ao{i}", [128, 4, 512], BF16) for i in range(2)]
            pcount = 0
            for bi, b0 in enumerate(range(0, nq, 4)):
                tiles = ext[b0:b0 + 4]
                nt = len(tiles)
                TB = nt * 128
                qT = qTs[bi % 2]
                ao = aos[bi % 2]
                for j, t in enumerate(tiles):
                    qi = b0 + j
                    ps1, ps2 = P.ps(), P.ps()
                    P.mm(ps1, [(ps1.ap[:, 0:480], [(cqT.ap[:, c, qi * 128:(qi + 1) * 128], w_uq.ap[:, c, 0:480]) for c in range(3)])],
                         r=[cqT, w_uq])
                    P.mm(ps2, [(ps2.ap[:, 0:288], [(cqT.ap[:, c, qi * 128:(qi + 1) * 128], w_uq.ap[:, c, 480:768]) for c in range(3)])],
                         r=[cqT, w_uq])
                    P.CP("act", qsb.ap[:, 0:480], ps1.ap[:, 0:480], [ps1], [qsb])
                    P.CP("act", qsb.ap[:, 480:768], ps2.ap[:, 0:288], [ps2, qsb], [qsb])
                    P.TT("dve", qsq.ap, qsb.ap, qsb.ap, ALU.mult, [qsb], [qsq])
                    P.RED(ssq8.ap, qsq.ap.rearrange("p (h d) -> p h d", h=8), [qsq], [ssq8])
                    P.rstd(ssq8, ssq8.ap, rq8, rq8.ap, 1.0 / 96.0, t8, t8.ap)
                    P.TT("dve", qsq.ap, qsb.ap, qg_bc, ALU.mult, [qsb, vecm, qsq], [qsq])
                    P.TT("dve", qn.ap, qsq.ap.rearrange("p (h d) -> p h d", h=8), rq8.ap.unsqueeze(2).to_broadcast([128, 8, 96]),
                         ALU.mult, [qsq, rq8], [qn])
                    if job["rope"]:
                        rope_apply(qn, qn.ap[:, :, 64:96], qrp, qrp.ap, t, 8, qrt1, qrt2)
                        P.CP("dve", qbf.ap[:, :, 0:64], qn.ap[:, :, 0:64], [qn], [qbf])
                        P.CP("dve", qbf.ap[:, :, 64:96], qrp.ap, [qrp, qbf], [qbf])
                    else:
                        P.CP("dve", qbf.ap, qn.ap, [qn], [qbf])
                    pst = P.ps()
                    ptv = pst.ap.bitcast(BF16).rearrange("p (h t) -> p h t", h=8)
                    P.tr(pst, [(ptv[0:96, h, :], qbf.ap[:, h, :]) for h in range(8)], r=[qbf, identb], ident=identb.ap)
                    P.CP("act", qT.ap[:, :, j * 128:(j + 1) * 128], ptv[0:96], [pst], [qT])
                items = [(h, kc) for h in range(8) for kc in range(nk)]
                nit = len(items)
                pss_l = [None] * nit

                def qk(i):
                    h, kc = items[i]
                    pss = P.ps()
                    P.mm(pss, [(pss.ap[:, 0:TB], [(KT.ap[:, h, kc * 128:(kc + 1) * 128], qT.ap[:, h, 0:TB])])], r=[KT, qT])
                    pss_l[i] = pss
                qk(0)
                for i in range(nit):
                    h, kc = items[i]
                    pso = P.psB[h % 2]
                    ov = pso.ap[:, 0:nt * 65].rearrange("p (t d) -> p t d", t=nt)
                    pss = pss_l[i]
                    PT = PTs[pcount % 3]
                    pcount += 1
                    P.ACT(PT.ap[:, 0:TB], pss.ap[:, 0:TB], AF.Exp, [pss], [PT], scale=SM_SCALE)
                    if i + 1 < nit:
                        qk(i + 1)
                    pv_op(PT, kc, ov, h, nt, nk, pso)
                    if kc == nk - 1:
                        P.RECIP(rc.ap[:, 0:nt], ov[:, :, 64], [pso], [rc])
                        P.TT("dve", ao.ap[:, 0:nt, h * 64:(h + 1) * 64], ov[:, :, 0:64],
                             rc.ap[:, 0:nt].unsqueeze(2).to_broadcast([128, nt, 64]), ALU.mult, [pso, rc], [ao])
                row = job["row0"] + b0 * 128
                P.store(ATT.ap[row:row + TB, :].rearrange("(t p) f -> p t f", p=128), ao, eng="pool", src_ap=ao.ap[:, 0:nt, :], w=[ATT])
            P.free(qsb, qsq, qn, qrp, qrt1, qrt2, qbf, ssq8, rq8, t8, rc, *qTs, *PTs, *aos)

        for job in jobs:
            S.phase = 'mla_build_' + job["name"]
            nk = build_keys(job)
            if job["g"] == 1 or not RUN_SAMPLE and job["seq"] == 3:
                P.free(w_mla, w_ukv)
                stg_keep.extend(precast_ffn())
            S.phase = 'mla_attn_' + job["name"]
            attention(job, nk)
        P.free(*stg_keep)
        P.free(w_uq, vecm, KT, VA, cqT)

    S.phase = 'mla'
    mla_pass()

    def gla_pass():
        w_gfm = P.T("w_gfm", [128, 8, 1056], BF16)
        P.load(w_gfm, I["w_gfm"], eng="pool")
        w_gtm = P.T("w_gtm", [128, 8, 768], BF16)
        P.load(w_gtm, I["w_gtm"], eng="pool")
        walb = P.T("walb", [33, 2, 2, 256])
        P.load(walb, I["wal"].rearrange("g k d n -> k g d n"))
        gn_ap = fm.ap[:, 0:1]
        bsets_g = []
        for i_ in range(2):
            hT_ = P.T(f"g_hT{i_}", [128, 8, 512], BF16)
            gqT_ = P.T(f"g_gqT{i_}", [64, 4, 512])
            gkT_ = P.T(f"g_gkT{i_}", [64, 4, 512])
            gaT_ = P.T(f"g_gaT{i_}", [33, 512])
            P.MEMSET("dve", gaT_.ap[32:33, :], 1.0, [gaT_])
            bsets_g.append((hT_, gqT_, gkT_, gaT_))
        bcur = [0]
        sgg = P.T("g_sgg", [128, 4, 512])
        class TS_:
            pass
        tsets = []
        for i_ in range(2):
            t_ = TS_()
            t_.gktm = P.T(f"g_gktm{i_}", [128, 256])
            t_.gvb = P.T(f"g_gvb{i_}", [128, 512], BF16)
            t_.ez = P.T(f"g_ez{i_}", [128, 256])
            t_.lz = P.T(f"g_lz{i_}", [128, 256])
            t_.E = P.T(f"g_E{i_}", [64, 4, 128])
            t_.En = P.T(f"g_En{i_}", [64, 4, 128])
            t_.Es = P.T(f"g_Es{i_}", [128, 256])
            t_.qt = P.T(f"g_qt{i_}", [64, 4, 128], BF16)
            t_.kt = P.T(f"g_kt{i_}", [64, 4, 128], BF16)
            t_.kh = P.T(f"g_kh{i_}", [128, 256], BF16)
            t_.Ab = P.T(f"g_Ab{i_}", [128, 4, 128], BF16)
            tsets.append(t_)
        tcnt = [0]
        Sst = P.T("g_S", [64, 4, 128])
        Sb = P.T("g_Sb", [64, 4, 128], BF16)
        St = P.T("g_St", [64, 4, 128])
        NE = NST if RUN_SAMPLE else 2
        ost = P.T("g_ost", [128, 4, NE * 128])
        osq = P.T("g_osq", [128, 512], BF16)
        rsd = P.T("g_rsd", [128, 512])
        rt1 = P.T("g_rt1", [128, 512])
        gl = P.T("g_gl", [128, 4, 512], BF16)
        pso = P.psB[0]
        pov = pso.ap.rearrange("p (h t) -> p h t", h=4)

        def block_prep(job, A_, B_, tiles, d, full):
            bcur[0] += 1
            hT, gqT, gkT, gaT = bsets_g[bcur[0] % 2]
            TB = len(tiles) * 128
            for j, t in enumerate(tiles):
                P.load(hT, HTS.ap[t if job["g"] == 0 else NPT + t], r=[HTS], dst_ap=hT.ap[:, :, j * 128:(j + 1) * 128])
            if full:
                for (dst, off) in ((gqT, 0), (gkT, 256)):
                    for h in range(4):
                        psb = P.ps()
                        P.mm(psb, [(psb.ap[0:64, 0:TB], [(w_gfm.ap[:, c, off + h * 64: off + (h + 1) * 64], hT.ap[:, c, 0:TB])
                                                         for c in range(8)])], r=[w_gfm, hT])
                        P.CP("act", dst.ap[:, h, 0:TB], psb.ap[0:64, 0:TB], [psb], [dst])
            psb = P.ps()
            P.mm(psb, [(psb.ap[0:32, 0:TB], [(w_gfm.ap[:, c, 1024:1056], hT.ap[:, c, 0:TB]) for c in range(8)])], r=[w_gfm, hT])
            P.CP("dve", gaT.ap[0:32, 0:TB], psb.ap[0:32, 0:TB], [psb], [gaT])
            if full and d == "f":
                for h in range(4):
                    psb = P.ps()
                    P.mm(psb, [(psb.ap[:, 0:TB], [(w_gfm.ap[:, c, 512 + h * 128: 512 + (h + 1) * 128], hT.ap[:, c, 0:TB])
                                                  for c in range(8)])], r=[w_gfm, hT])
                    P.ACT(sgg.ap[:, h, 0:TB], psb.ap[:, 0:TB], AF.Silu, [psb], [sgg])

        def tile_prep(job, j, d, full):
            tcnt[0] += 1
            ts = tsets[tcnt[0] % 2]
            gktm, gvb, ez, lz, E, En, Es, qt, kt, kh = ts.gktm, ts.gvb, ts.ez, ts.lz, ts.E, ts.En, ts.Es, ts.qt, ts.kt, ts.kh
            hT, gqT, gkT, gaT = bsets_g[bcur[0] % 2]
            g = job["g"]
            di = 0 if d == "f" else 1
            c0 = j * 128
            psk, psv = P.ps(), P.ps()
            P.mm(psk, [(psk.ap[:, 0:256], [(hT.ap[:, c, c0:c0 + 128], w_gtm.ap[:, c, 0:256]) for c in range(8)])], r=[hT, w_gtm])
            P.mm(psv, [(psv.ap, [(hT.ap[:, c, c0:c0 + 128], w_gtm.ap[:, c, 256:768]) for c in range(8)])], r=[hT, w_gtm])
            P.CP("dve", gktm.ap, psk.ap[:, 0:256], [psk], [gktm])
            P.CP("act", gvb.ap, psv.ap, [psv], [gvb])
            psz = P.ps()
            P.mm(psz, [(psz.ap[:, 0:256], [(gaT.ap[:, c0:c0 + 128], walb.ap[:, g, di, :])])], r=[gaT, walb])
            P.ACT(ez.ap, psz.ap[:, 0:256], AF.Exp, [psz], [ez], scale=-1.0)
            P.ACT(lz.ap, ez.ap, AF.Ln, [ez, cvec], [lz], bias=cvec.ap[:, 2:3], scale=1.0)
            psc = P.ps()
            pcv = psc.ap[0:64, :].rearrange("p (h t) -> p h t", h=4)
            P.mm(psc, [(pcv[:, h, :], [(lz.ap[:, h * 64:(h + 1) * 64], TRI[(d, 0)])]) for h in range(4)], r=[lz, cst])
            pss = P.ps()
            P.mm(pss, [(pss.ap[:, 0:256], [(TRI[(d, 1)], lz.ap)])], r=[lz, cst])
            P.ACT(E.ap, pcv, AF.Exp, [psc], [E])
            P.ACT(Es.ap, pss.ap[:, 0:256], AF.Exp, [pss], [Es])
            P.TT("dve", kh.ap, gktm.ap, Es.ap, ALU.mult, [gktm, Es], [kh])
            if full:
                P.ACT(En.ap, pcv, AF.Exp, [psc], [En], scale=-1.0)
                P.STT("dve", qt.ap, gqT.ap[:, :, c0:c0 + 128], 0.125, E.ap, ALU.mult, ALU.mult, [gqT, E], [qt])
                P.TT("dve", kt.ap, gkT.ap[:, :, c0:c0 + 128], En.ap, ALU.mult, [gkT, En], [kt])

        def o_op(ci, r0, gvb, Ab, qt):
            def fn(e):
                last = None
                for h in range(4):
                    if ci == 0:
                        last = e.matmul(pov[:, h, :], lhsT=gvb.ap[:, h * 128:(h + 1) * 128], rhs=Ab.ap[:, h, :], start=(h == 0), stop=False,
                                        skip_group_check=True)
                    last = e.matmul(pov[:, h, r0:r0 + 64], lhsT=Sb.ap[:, h, :], rhs=qt.ap[:, h, r0:r0 + 64], start=False, stop=(ci == 1),
                                    skip_group_check=True)
                return last
            P.S.op("pe", fn, [gvb, Ab, Sb, qt], [pso], name="o_op", cost=0.07 * 8, lat=0.1)

        def tile_scan(d, full, ocol):
            ts = tsets[tcnt[0] % 2]
            gvb, E, qt, kt, kh, Ab = ts.gvb, ts.E, ts.qt, ts.kt, ts.kh, ts.Ab
            if full:
                psa = P.ps()
                pav = psa.ap.rearrange("p (h t) -> p h t", h=4)
                P.mm(psa, [(pav[:, h, :], [(kt.ap[:, h, :], qt.ap[:, h, :])]) for h in range(4)], r=[kt, qt])
                P.TT("dve", Ab.ap, pav, MASK[d].unsqueeze(1).to_broadcast([128, 4, 128]), ALU.mult, [psa, cst], [Ab])
            chunks = (0, 1) if d == "f" else (1, 0)
            for ci, c in enumerate(chunks):
                r0 = c * 64
                if full:
                    o_op(ci, r0, gvb, Ab, qt)
                psn = P.ps()
                pnv = psn.ap[0:64, :].rearrange("p (h v) -> p h v", h=4)
                P.mm(psn, [(pnv[:, h, :], [(kh.ap[r0:r0 + 64, h * 64:(h + 1) * 64], gvb.ap[r0:r0 + 64, h * 128:(h + 1) * 128])])
                           for h in range(4)], r=[kh, gvb])
                col = (r0 + 63) if d == "f" else r0
                dec = E.ap[:, :, col:col + 1].to_broadcast([64, 4, 128])
                P.TT("dve", St.ap, Sst.ap, dec, ALU.mult, [Sst, E], [St])
                P.TT("dve", Sst.ap, St.ap, pnv, ALU.add, [St, psn], [Sst])
                P.CP("act", Sb.ap, Sst.ap, [Sst], [Sb])
            if full:
                if d == "b":
                    P.CP("act", ost.ap[:, :, ocol:ocol + 128], pov, [pso], [ost])
                else:
                    P.TT("dve", ost.ap[:, :, ocol:ocol + 128], pov, ost.ap[:, :, ocol:ocol + 128], ALU.add, [pso, ost], [ost])

        def finalize_block(job, b0, nt):
            TB = nt * 128
            for h in range(4):
                ov = ost.ap[:, h, b0 * 128:b0 * 128 + TB]
                P.ACT(osq.ap[:, 0:TB], ov, AF.Square, [ost], [osq])
                psb = P.ps()
                P.mm(psb, [(psb.ap[:, 0:TB], [(onesb.ap, osq.ap[:, 0:TB])])], r=[onesb, osq])
                P.rstd(psb, psb.ap[:, 0:TB], rsd, rsd.ap[:, 0:TB], 1.0 / 128.0, rt1, rt1.ap[:, 0:TB])
                P.TT("dve", rt1.ap[:, 0:TB], ov, rsd.ap[:, 0:TB], ALU.mult, [ost, rsd, rt1], [rt1])
                P.STT("dve", gl.ap[:, h, 0:TB], rt1.ap[:, 0:TB], gn_ap, sgg.ap[:, h, 0:TB], ALU.mult, ALU.mult, [rt1, fm, sgg], [gl])
            row = job["row0"] + b0 * 128
            P.store(GLT.ap[:, :, row:row + TB].rearrange("h p t -> p h t"), gl, eng="pool", src_ap=gl.ap[:, :, 0:TB], w=[GLT])

        def init_state(job, di):
            if job["g"] == 0:
                P.MEMSET("dve", Sst.ap, 0.0, [Sst])
            else:
                P.load(Sst, I["st0"][di].rearrange("h d v -> d h v"))
            P.CP("act", Sb.ap, Sst.ap, [Sst], [Sb])

        cur_g = None
        AB = None
        for job in jobs:
            g = job["g"]
            if cur_g != g:
                if AB is not None:
                    P.free(*AB)
                m = load_mod(0, g, [0, 1], "")
                make_A(m[1], I["v_nm"][0:1, :], "mod")
                AB = (m[1], m[0])
                cur_g = g
            A_, B_ = AB
            ext, other = job["ext"], job["other"]
            init_state(job, 1)
            ob = [other[i:i + 4] for i in range(0, len(other), 4)]
            for tiles in reversed(ob):
                block_prep(job, A_, B_, tiles, "b", False)
                for j in reversed(range(len(tiles))):
                    tile_prep(job, j, "b", False)
                    tile_scan("b", False, None)
            eb = [(i, ext[i:i + 4]) for i in range(0, len(ext), 4)]
            for (b0, tiles) in reversed(eb):
                block_prep(job, A_, B_, tiles, "b", True)
                for j in reversed(range(len(tiles))):
                    tile_prep(job, j, "b", True)
                    tile_scan("b", True, (b0 + j) * 128)
            if g == 0:
                P.store(O["st_o"][job["seq"], 1].rearrange("h d v -> d h v"), Sst, final=True)
            init_state(job, 0)
            for (b0, tiles) in eb:
                block_prep(job, A_, B_, tiles, "f", True)
                for j in range(len(tiles)):
                    tile_prep(job, j, "f", True)
                    tile_scan("f", True, (b0 + j) * 128)
                finalize_block(job, b0, len(tiles))
            if g == 0:
                P.store(O["st_o"][job["seq"], 0].rearrange("h d v -> d h v"), Sst, final=True)
        P.free(*AB)
        for t_ in tsets:
            P.free(t_.gktm, t_.gvb, t_.ez, t_.lz, t_.E, t_.En, t_.Es, t_.qt, t_.kt, t_.kh, t_.Ab)
        for bs_ in bsets_g:
            P.free(*bs_)
        P.free(w_gfm, w_gtm, walb, sgg, Sst, Sb, St, ost, osq, rsd, rt1, gl)

    S.phase = 'gla'
    gla_pass()

    def mk_ffn_bufs(pfx):
        wgs = [P.T(f"{pfx}wg{i}", [128, 8, 256], BF16) for i in range(3)]
        wus = [P.T(f"{pfx}wu{i}", [128, 8, 256], BF16) for i in range(3)]
        wds = [P.T(f"{pfx}wd{i}", [128, 2, 1024], BF16) for i in range(3)]
        sg2 = [P.T(f"{pfx}sg{i}", [128, 512]) for i in range(2)]
        return (wgs, wus, wds, sg2)

    def free_ffn_bufs(wb):
        P.free(*wb[0], *wb[1], *wb[2], *wb[3])

    def ffn(l, subs, hTb, actT, wb, tile_out):
        wgs, wus, wds, sg2 = wb
        for gi in range(NG):
            wg, wu = wgs[gi % 3], wus[gi % 3]
            P.load(wg, WG[l].ap[gi], r=[WG[l]])
            P.load(wu, WU[l].ap[gi], r=[WU[l]])
            for jj in range(2):
                jx = gi * 2 + jj
                for (c0, nt) in subs:
                    TB = nt * 128
                    pg, pu = P.ps(), P.ps()
                    P.mm(pg, [(pg.ap[:, 0:TB], [(wg.ap[:, c, jj * 128:(jj + 1) * 128], hTb.ap[:, c, c0:c0 + TB]) for c in range(8)])],
                         r=[wg, hTb])
                    P.mm(pu, [(pu.ap[:, 0:TB], [(wu.ap[:, c, jj * 128:(jj + 1) * 128], hTb.ap[:, c, c0:c0 + TB]) for c in range(8)])],
                         r=[wu, hTb])
                    sg = sg2[jx % 2]
                    P.ACT(sg.ap[:, 0:TB], pg.ap[:, 0:TB], AF.Silu, [pg], [sg])
                    P.TT("dve", actT.ap[:, jx, c0:c0 + TB], sg.ap[:, 0:TB], pu.ap[:, 0:TB], ALU.mult, [sg, pu], [actT])
        allps = S.psum_bufs
        wcount = 0
        for si, (c0, nt) in enumerate(subs):
            accs = [(allps[2 * j], allps[2 * j + 1]) for j in range(nt)]
            used = [b for pr in accs for b in pr]
            for gi in range(NG):
                wd = wds[wcount % 3]
                wcount += 1
                P.load(wd, WD[l].ap[gi], r=[WD[l]])
                groups = []
                for j in range(nt):
                    cc = c0 + j * 128
                    for hf in range(2):
                        pa = accs[j][hf].ap
                        pairs = [(actT.ap[:, gi * 2 + jj, cc:cc + 128], wd.ap[:, jj, hf * 512:(hf + 1) * 512]) for jj in range(2)]
                        groups.append((pa, pairs, gi == 0, gi == NG - 1))
                P.mm(used, groups, r=[actT, wd], name="ffn_down")
            for j in range(nt):
                tile_out(si, j, accs[j][0], accs[j][1])

    S.phase = 'ada1'
    ada_layer(1, 0)
    ada_layer(1, 1)

    def l0c_pass():
        w_out = P.T("w_out", [128, 8, D], BF16)
        P.load(w_out, I["w_out"], eng="pool")
        pw1 = P.T("pw1", [128, 8, 2 * D], BF16)
        P.load(pw1, I["pw1"], eng="pool")
        wb = mk_ffn_bufs("f_")
        wk = mk_wk("c_")
        MAXT = 5
        x1s = [P.T(f"xr_{i}", [128, D]) for i in range(MAXT)]
        hTb = P.T("c_hTb", [128, 8, MAXT * 128], BF16)
        actT = P.T("c_actT", [128, NJ, MAXT * 128], BF16)
        att = P.T("c_att", [128, 512], BF16)
        ccT = P.T("c_ccT", [128, 8, 128], BF16)
        tmp = wk["tmp"]
        usb = [P.T(f"c_u{i}", [128, 512]) for i in range(2)]
        sgu = [P.T(f"c_sgu{i}", [128, 512]) for i in range(2)]
        blocks = [(0, [(0, 4)]), (0, [(4, 4)])]
        if RUN_SAMPLE:
            for k in range(3):
                blocks.append((1, [(NPT + 4 * k, 4)]))
            blocks.append((1, [(NPT + 12, 4), (NPT + 16, 1)]))
        cur_g = None
        M = {}
        ucnt = 0
        for (g, sbs) in blocks:
            if cur_g != g:
                if M:
                    P.free(*M.values())
                    M = {}
                m0 = load_mod(0, g, [2, 3, 4, 5], "")
                make_A(m0[4], I["v_nf"][0:1, :], "mod")
                m1 = load_mod(1, g, [0, 1], "n")
                make_A(m1[1], I["v_nm"][1:2, :], "mod")
                M = dict(G1=m0[2], B2=m0[3], A2=m0[4], G2=m0[5], B1n=m1[0], A1n=m1[1])
                cur_g = g
            xsrc = I["xp"] if g == 0 else I["xs"]
            subs = []
            col = 0
            tl = []
            for (t0, nt) in sbs:
                subs.append((col, nt))
                for j in range(nt):
                    tl.append((t0 + j, col + j * 128))
                col += nt * 128
            for k, (st, cc) in enumerate(tl):
                x1 = x1s[k]
                lt = st if g == 0 else st - NPT
                P.load(x1, xsrc[lt * 128:(lt + 1) * 128, :])
                P.load(att, ATT.ap[st * 128:(st + 1) * 128, :], r=[ATT])
                pst = P.ps()
                pv = pst.ap.bitcast(BF16)[:, 0:512].rearrange("p (c t) -> p c t", c=4)
                P.tr(pst, [(pv[:, c, :], att.ap[:, c * 128:(c + 1) * 128]) for c in range(4)], r=[att, identb], ident=identb.ap)
                P.CP("act", ccT.ap[:, 0:4, :], pv, [pst], [ccT])
                P.load(ccT, GLT.ap[:, :, st * 128:(st + 1) * 128].rearrange("h p t -> p h t"), r=[GLT], dst_ap=ccT.ap[:, 4:8, :])
                p0, p1 = P.ps(), P.ps()
                P.mm(p0, [(p0.ap, [(ccT.ap[:, c, :], w_out.ap[:, c, 0:512]) for c in range(8)])], r=[ccT, w_out])
                P.mm(p1, [(p1.ap, [(ccT.ap[:, c, :], w_out.ap[:, c, 512:1024]) for c in range(8)])], r=[ccT, w_out])
                for hf, pp in ((0, p0), (1, p1)):
                    P.TT("dve", tmp.ap[:, hf * 512:(hf + 1) * 512], pp.ap, M["G1"].ap[:, hf * 512:(hf + 1) * 512], ALU.mult, [pp, M["G1"]], [tmp])
                P.TT("dve", x1.ap, x1.ap, tmp.ap, ALU.add, [x1, tmp], [x1])
                tm_norm_mod(x1, M["A2"], M["B2"], hTb, hTb.ap[:, :, cc:cc + 128], wk)

            def tile_out(si, j, p0, p1, subs=subs, sbs=sbs, g=g, M=M):
                k = sum(n for (_, n) in subs[:si]) + j
                x1 = x1s[k]
                st = sbs[si][0] + j
                for hf, pp in ((0, p0), (1, p1)):
                    P.TT("dve", tmp.ap[:, hf * 512:(hf + 1) * 512], pp.ap, M["G2"].ap[:, hf * 512:(hf + 1) * 512], ALU.mult, [pp, M["G2"]], [tmp])
                P.TT("dve", x1.ap, x1.ap, tmp.ap, ALU.add, [x1, tmp], [x1])
                if not (g == 1 and st == NPT + 16):
                    P.store(X2.ap[st * 128:(st + 1) * 128, :], x1, eng="pool", w=[X2])
            ffn(0, subs, hTb, actT, wb, tile_out)
            for k, (st, cc) in enumerate(tl):
                tm_norm_mod(x1s[k], M["A1n"], M["B1n"], hTb, hTb.ap[:, :, cc:cc + 128], wk)
            for (c0, nt), (t0, _) in zip(subs, sbs):
                TB = nt * 128
                for c in range(8):
                    pv_, pg_ = P.ps(), P.ps()
                    P.mm(pv_, [(pv_.ap[:, 0:TB], [(pw1.ap[:, k, c * 128:(c + 1) * 128], hTb.ap[:, k, c0:c0 + TB]) for k in range(8)])],
                         r=[pw1, hTb])
                    P.mm(pg_, [(pg_.ap[:, 0:TB], [(pw1.ap[:, k, D + c * 128:D + (c + 1) * 128], hTb.ap[:, k, c0:c0 + TB]) for k in range(8)])],
                         r=[pw1, hTb])
                    sg = sgu[ucnt % 2]
                    ub = usb[ucnt % 2]
                    ucnt += 1
                    P.ACT(sg.ap[:, 0:TB], pg_.ap[:, 0:TB], AF.Sigmoid, [pg_, fm], [sg], bias=fm.ap[:, 9 + c:10 + c], scale=1.0)
                    P.STT("dve", ub.ap[:, 0:TB], pv_.ap[:, 0:TB], fm.ap[:, 1 + c:2 + c], sg.ap[:, 0:TB], ALU.add, ALU.mult, [pv_, sg, fm], [ub])
                    P.store(UT.ap[c * 128:(c + 1) * 128, t0 * 128:t0 * 128 + TB], ub, eng="pool", src_ap=ub.ap[:, 0:TB], w=[UT])
        P.free(*M.values())
        free_wk(wk)
        free_ffn_bufs(wb)
        P.free(w_out, pw1, *x1s, hTb, actT, att, ccT, *usb, *sgu)

    S.phase = 'l0c'
    l0c_pass()

    def l1d_pass():
        pw2 = P.T("pw2", [128, 8, D], BF16)
        P.load(pw2, I["pw2"], eng="pool")
        cdw = P.T("cdw", [128, 2, 8, 31])
        P.load(cdw, I["cdw"])
        bpw2 = P.T("bpw2", [128, D])
        P.load(bpw2, I["vec"][0:1, 2048:3072].partition_broadcast(128))
        wb = mk_ffn_bufs("f_")
        wk = mk_wk("d_")
        x2s = [P.T(f"xr_{i}", [128, D]) for i in range(4)]
        hTb = P.T("d_hTb", [128, 8, 512], BF16)
        actT = P.T("d_actT", [128, NJ, 512], BF16)
        ups = [P.T(f"d_up{i}", [128, 512 + 32], BF16) for i in range(3)]
        dgs = [P.T(f"d_dg{i}", [128, 31, 128], BF16) for i in range(2)]
        upfs = [P.T(f"d_upf{i}", [128, 512 + 32]) for i in range(2)]
        fcnt = 0

        def conv_op(psb, pa, dg, up, ln, first):
            def fn(e):
                last = None
                for k in range(31):
                    last = e.matmul(pa, lhsT=dg.ap[:, k, :], rhs=up.ap[:, k:k + ln], start=(first and k == 0), stop=(k == 30),
                                    skip_group_check=True)
                return last
            P.S.op("pe", fn, [dg, up], [psb], name="conv", cost=31 * (ln / 2000.0 + 0.03), lat=0.1)
        cv = P.T("d_cv", [128, 8, 512])
        csqs = [P.T(f"d_csq{i}", [128, 512]) for i in range(2)]
        mu = P.T("d_mu", [128, 512])
        var = P.T("d_var", [128, 512])
        rsd = P.T("d_rsd", [128, 512])
        tts = [P.T(f"d_tt{i}", [128, 512]) for i in range(2)]
        onesf = P.T("d_onesf", [128, 128])
        P.MEMSET("dve", onesf.ap, 1.0 / D, [onesf])
        aT = P.T("d_aT", [128, 8, 512], BF16)
        tmp = wk["tmp"]
        blocks = []
        for k in range(2):
            blocks.append((0, 4 * k, [(0, 256, False, False), (256, 256, False, False)], O["yp"], 512 * k))
        if RUN_SAMPLE:
            for k in range(4):
                blocks.append((1, NPT + 4 * k, [(0, 512, k > 0, True)], O["ys"], 512 * k))
        cur_g = None
        M = {}
        ucnt = 0
        for (g, t0, pieces, ydst, yrow) in blocks:
            if cur_g != g:
                if M:
                    P.free(*M.values())
                    M = {}
                m1 = load_mod(1, g, [2, 3, 4, 5], "")
                make_A(m1[4], I["v_nf"][1:2, :], "mod")
                M = dict(G1=m1[2], B2=m1[3], A2=m1[4], G2=m1[5])
                cur_g = g
            base = t0 * 128
            for c in range(8):
                if c in DVE_CONV_CHUNKS:
                    for pi, (c0, ln, lh, rh) in enumerate(pieces):
                        upf = upfs[fcnt % 2]
                        fcnt += 1
                        lo = base + c0 - (15 if lh else 0)
                        hi = base + c0 + ln + (15 if rh else 0)
                        dlo = 0 if lh else 15
                        if not lh:
                            P.MEMSET("dve", upf.ap[:, 0:15], 0.0, [upf])
                        if not rh:
                            P.MEMSET("dve", upf.ap[:, 15 + ln:30 + ln], 0.0, [upf])
                        P.load(upf, UT.ap[c * 128:(c + 1) * 128, lo:hi], r=[UT], dst_ap=upf.ap[:, dlo:dlo + (hi - lo)])
                        dst = cv.ap[:, c, c0:c0 + ln]
                        P.TS("dve", dst, upf.ap[:, 0:ln], cdw.ap[:, g, c, 0:1], fm.ap[:, 17 + c:18 + c], ALU.mult, ALU.add, [upf, cdw, fm], [cv])
                        for k in range(1, 31):
                            P.STT("dve", dst, upf.ap[:, k:k + ln], cdw.ap[:, g, c, k:k + 1], dst, ALU.mult, ALU.add, [upf, cdw, cv], [cv])
                    continue
                dg = dgs[c % 2]
                P.TT("dve", dg.ap, identb.ap.unsqueeze(1).to_broadcast([128, 31, 128]),
                     cdw.ap[:, g, c, :].unsqueeze(2).to_broadcast([128, 31, 128]), ALU.mult, [identb, cdw], [dg])
                psc_ = P.ps()
                for pi, (c0, ln, lh, rh) in enumerate(pieces):
                    up = ups[ucnt % 3]
                    ucnt += 1
                    lo = base + c0 - (15 if lh else 0)
                    hi = base + c0 + ln + (15 if rh else 0)
                    dlo = 0 if lh else 15
                    if not lh:
                        P.MEMSET("dve", up.ap[:, 0:15], 0.0, [up])
                    if not rh:
                        P.MEMSET("dve", up.ap[:, 15 + ln:30 + ln], 0.0, [up])
                    P.load(up, UT.ap[c * 128:(c + 1) * 128, lo:hi], eng="pool", r=[UT], dst_ap=up.ap[:, dlo:dlo + (hi - lo)])
                    conv_op(psc_, psc_.ap[:, c0:c0 + ln], dg, up, ln, pi == 0)
                P.ACT(cv.ap[:, c, :], psc_.ap, AF.Identity, [psc_, fm], [cv], bias=fm.ap[:, 17 + c:18 + c], scale=1.0)
            psm, psq = P.ps(), P.ps()
            P.mm(psm, [(psm.ap, [(onesf.ap, cv.ap[:, c, :]) for c in range(8)])], r=[onesf, cv])
            for c in range(8):
                csq = csqs[c % 2]
                P.ACT(csq.ap, cv.ap[:, c, :], AF.Square, [cv], [csq])
                P.mm(psq, [(psq.ap, [(onesf.ap, csq.ap)], c == 0, c == 7)], r=[onesf, csq])
            P.CP("act", mu.ap, psm.ap, [psm], [mu])
            P.TT("dve", var.ap, mu.ap, mu.ap, ALU.mult, [mu], [var])
            P.TT("dve", var.ap, psq.ap, var.ap, ALU.subtract, [psq, var], [var])
            P.rstd(var, var.ap, rsd, rsd.ap, 1.0, tts[0], tts[0].ap)
            for c in range(8):
                tt = tts[c % 2]
                P.TT("dve", tt.ap, cv.ap[:, c, :], mu.ap, ALU.subtract, [cv, mu], [tt])
                P.TT("dve", tt.ap, tt.ap, rsd.ap, ALU.mult, [tt, rsd], [tt])
                P.ACT(aT.ap[:, c, :], tt.ap, AF.Silu, [tt, fm], [aT], scale=fm.ap[:, 25 + c:26 + c], bias=fm.ap[:, 33 + c:34 + c])
            for j in range(4):
                x2 = x2s[j]
                P.load(x2, X2.ap[base + j * 128: base + (j + 1) * 128, :], r=[X2])
                p0, p1 = P.ps(), P.ps()
                P.mm(p0, [(p0.ap, [(aT.ap[:, c, j * 128:(j + 1) * 128], pw2.ap[:, c, 0:512]) for c in range(8)])], r=[aT, pw2])
                P.mm(p1, [(p1.ap, [(aT.ap[:, c, j * 128:(j + 1) * 128], pw2.ap[:, c, 512:1024]) for c in range(8)])], r=[aT, pw2])
                for hf, pp in ((0, p0), (1, p1)):
                    P.TT("dve", tmp.ap[:, hf * 512:(hf + 1) * 512], pp.ap, bpw2.ap[:, hf * 512:(hf + 1) * 512], ALU.add, [pp, bpw2], [tmp])
                P.TT("dve", tmp.ap, tmp.ap, M["G1"].ap, ALU.mult, [tmp, M["G1"]], [tmp])
                P.TT("dve", x2.ap, x2.ap, tmp.ap, ALU.add, [x2, tmp], [x2])
                tm_norm_mod(x2, M["A2"], M["B2"], hTb, hTb.ap[:, :, j * 128:(j + 1) * 128], wk)

            def tile_out(si, j, p0, p1, ydst=ydst, yrow=yrow, M=M):
                x2 = x2s[j]
                for hf, pp in ((0, p0), (1, p1)):
                    P.TT("dve", tmp.ap[:, hf * 512:(hf + 1) * 512], pp.ap, M["G2"].ap[:, hf * 512:(hf + 1) * 512], ALU.mult, [pp, M["G2"]], [tmp])
                P.TT("dve", x2.ap, x2.ap, tmp.ap, ALU.add, [x2, tmp], [x2])
                P.store(ydst[yrow + j * 128: yrow + (j + 1) * 128, :], x2, eng="pool", final=True)
            ffn(1, [(0, 4)], hTb, actT, wb, tile_out)
        P.free(*M.values())
        free_wk(wk)
        free_ffn_bufs(wb)
        P.free(pw2, cdw, bpw2, *x2s, hTb, actT, *ups, cv, *csqs, mu, var, rsd, *tts, onesf, aT, *dgs, *upfs)

    S.phase = 'l1d'
    l1d_pass()

    fin = S.op("sp", lambda e: None, name="final")
    fin.deps.extend(P.outstores)
    return P


OUT_SPECS = {
    "yp": ([1024, D], F32), "ys": ([2048, D], F32), "ckv_o": ([1024, 256], F32), "kr_o": ([1024, 32], F32),
    "st_o": ([4, 2, 4, 64, 128], F32),
}


def build_nc(in_map0):
    nc = bass.Bass("TRN2", target_bir_lowering=False)
    I = {k: nc.dram_tensor(k, list(v.shape), F32, kind="ExternalInput").ap() for k, v in in_map0.items()}
    O = {k: nc.dram_tensor(k, list(sh), dt, kind="ExternalOutput").ap() for k, (sh, dt) in OUT_SPECS.items()}
    with ExitStack() as es:
        arena = es.enter_context(nc.sbuf_tensor("arena", [128, ARENA_WORDS], F32))
        S = Sched(nc, ARENA_WORDS)
        S.arena = arena[:]
        for i in range(8):
            pt = es.enter_context(nc.psum_tensor(f"ps{i}", [128, 512], F32))
            S.psum_bufs.append(Buf(f"ps{i}", pt[:], is_psum=True))
        build_program(nc, S, I, O)
        def sem_alloc(name):
            return es.enter_context(nc.semaphore(name))
        if RESCHED:
            S.reschedule()
        S.finalize_plan(sem_alloc)
        block = es.enter_context(nc.Block())

        @block.sync
        def _(e):
            S.run_engine("sp", e)

        @block.scalar
        def _(e):
            S.run_engine("act", e)

        @block.vector
        def _(e):
            S.run_engine("dve", e)

        @block.gpsimd
        def _(e):
            S.run_engine("pool", e)

        @block.tensor
        def _(e):
            S.run_engine("pe", e)
    return nc, S


def kernel(**inputs):
    inp = {k: np.asarray(v, dtype=np.float32) for k, v in inputs.items()}
    shared = prep_shared(inp)
    in_maps = []
    for core in range(8):
        d = dict(shared)
        d.update(prep_core(core, inp))
        in_maps.append(d)
    nc, S = build_nc(in_maps[0])
    res = run_bass_kernel_spmd(nc, in_maps, core_ids=list(range(8)))
    R = res.results
    y_prompt = np.concatenate([R[c]["yp"].reshape(4, 256, D) for c in range(8)], axis=0)
    y_sample = np.zeros((4, 4096, D), np.float32)
    for c in range(8):
        b, rev = c // 2, c % 2 == 1
        ys = R[c]["ys"]
        if rev:
            y_sample[b, 2048:] = ys[::-1]
        else:
            y_sample[b, :2048] = ys
    ckv = np.concatenate([R[c]["ckv_o"].reshape(4, 1, 256, 256) for c in range(8)], axis=0)
    kr = np.concatenate([R[c]["kr_o"].reshape(4, 1, 256, 32) for c in range(8)], axis=0)
    st = np.concatenate([R[c]["st_o"].reshape(4, 1, 2, 4, 64, 128) for c in range(8)], axis=0)
    return (y_prompt.astype(np.float32), y_sample, ckv.astype(np.float32), kr.astype(np.float32), st.astype(np.float32))
```

```python
import math
import numpy as np
import concourse.bass as bass
import concourse.mybir as mybir
from concourse.bass_utils import run_bass_kernel_spmd
from contextlib import ExitStack

F32 = mybir.dt.float32
BF16 = mybir.dt.bfloat16
ALU = mybir.AluOpType
AF = mybir.ActivationFunctionType
AX = mybir.AxisListType

ENGS = ("sp", "act", "dve", "pool", "pe")
ARENA_WORDS = 52800


class Buf:
    __slots__ = ("name", "last_w", "readers", "dsem", "ap", "rng", "is_dram", "is_psum")

    def __init__(self, name, ap=None, is_dram=False, is_psum=False):
        self.name = name
        self.is_dram = is_dram
        self.is_psum = is_psum
        self.last_w = None
        self.readers = []
        self.dsem = None
        self.ap = ap
        self.rng = None


class Op:
    __slots__ = ("idx", "eng", "fn", "deps", "dma", "sem", "val", "signal", "name", "cost", "lat", "t0", "t1", "phase")


class Sched:
    def __init__(self, nc, arena_words):
        self.nc = nc
        self.ops = []
        self.arena_words = arena_words
        self.ghosts = []
        self.live = []
        self.dma_last = {}
        self.arena = None
        self.psum_bufs = []
        self.psum_rr = 0
        self.peak = 0

    def alloc(self, name, shape, dtype=F32):
        free = int(np.prod(shape[1:]))
        esz = 4 if dtype == F32 else 2
        words = (free * esz + 3) // 4
        self.live.sort(key=lambda t: t[0])
        pos = 0
        for (s, e, _) in self.live:
            if s - pos >= words:
                break
            pos = max(pos, e)
        if pos + words > self.arena_words:
            raise RuntimeError(f"SBUF arena OOM allocating {name} ({words} words); live="
                               f"{[(b.name, e - s) for s, e, b in self.live]}")
        self.peak = max(self.peak, pos + words)
        pp = getattr(self, "phase_peak", None)
        if pp is None:
            pp = self.phase_peak = {}
        ph = getattr(self, "phase", "")
        pp[ph] = max(pp.get(ph, 0), pos + words)
        ap = self.arena[:, pos:pos + words]
        if dtype != F32:
            ap = ap.bitcast(dtype)
            ap = ap[:, 0:free]
        if len(shape) > 2:
            names = " ".join(f"d{i}" for i in range(len(shape) - 1))
            kw = {f"d{i}": int(shape[i + 1]) for i in range(len(shape) - 1)}
            ap = ap.rearrange(f"p ({names}) -> p {names}", **kw)
        if shape[0] < 128:
            ap = ap[0:shape[0]]
        b = Buf(name, ap)
        b.rng = (pos, pos + words)
        inh = []
        for (s, e, ops) in self.ghosts:
            if s < pos + words and pos < e:
                inh.extend(ops)
        b.readers = list(dict.fromkeys(inh))
        self.live.append((pos, pos + words, b))
        return b

    def free(self, *bufs):
        for b in bufs:
            for i, (s, e, bb) in enumerate(self.live):
                if bb is b:
                    self.live.pop(i)
                    ops = list(b.readers)
                    if b.last_w is not None:
                        ops.append(b.last_w)
                    self.ghosts = [(gs, ge, go) for (gs, ge, go) in self.ghosts if not (gs >= s and ge <= e)]
                    if ops:
                        self.ghosts.append((s, e, ops))
                    break
            else:
                raise RuntimeError(f"free of non-live buf {b.name}")

    def op(self, eng, fn, reads=(), writes=(), dma=0, name="", cost=0.6, lat=0.0):
        o = Op()
        o.cost = cost
        o.lat = lat
        o.phase = getattr(self, "phase", "")
        o.idx = len(self.ops)
        o.eng = eng
        o.fn = fn
        o.dma = dma
        o.sem = None
        o.val = 0
        o.signal = False
        o.name = name
        deps = []
        for b in reads:
            if b.last_w is not None:
                deps.append(b.last_w)
            if b.is_psum:
                deps.extend(r_ for r_ in b.readers if r_.eng != eng)
        for b in writes:
            if b.last_w is not None:
                deps.append(b.last_w)
            deps.extend(b.readers)
        if dma:
            key = None
            for b in list(writes) + list(reads):
                if not b.is_dram:
                    key = b
                    break
            assert key is not None, name
            kname = key.name + ("_sw" if eng == "pool" else "")
            o.sem = kname
            prev = self.dma_last.get(kname)
            if prev is not None:
                deps.append(prev)
            self.dma_last[kname] = o
        seen = {}
        for d in deps:
            if d is o:
                continue
            seen[d.idx] = d
        o.deps = list(seen.values())
        for b in reads:
            b.readers.append(o)
        for b in writes:
            b.last_w = o
            b.readers = []
        self.ops.append(o)
        return o

    @staticmethod
    def _nowait(o, d):
        return o.eng == "pe" and d.eng == "pe" and not d.dma and not o.dma

    def reschedule(self, sync_lat=0.8):
        import heapq
        ops = self.ops
        n = len(ops)
        succ = [[] for _ in range(n)]
        ndep = [0] * n
        for o in ops:
            ndep[o.idx] = len(o.deps)
            for d in o.deps:
                succ[d.idx].append(o)
        bl = [0.0] * n
        for o in reversed(ops):
            m = 0.0
            for s_ in succ[o.idx]:
                extra = 0.0 if (s_.eng == o.eng and not o.dma) else sync_lat
                if bl[s_.idx] + extra > m:
                    m = bl[s_.idx] + extra
            bl[o.idx] = m + o.cost + o.lat
        ready_t = [0.0] * n
        pend = {e: [] for e in ENGS}
        avail = {e: [] for e in ENGS}
        for o in ops:
            if ndep[o.idx] == 0:
                heapq.heappush(pend[o.eng], (0.0, o.idx))
        t_free = {e: 0.0 for e in ENGS}
        order = []
        done = 0
        use_bl = PRIO_BL
        while done < n:
            best = None
            for e in ENGS:
                tf = t_free[e]
                pe_, av = pend[e], avail[e]
                while pe_ and pe_[0][0] <= tf:
                    rt, idx = heapq.heappop(pe_)
                    heapq.heappush(av, ((-bl[idx] if use_bl else idx), idx))
                if av:
                    st = tf
                    idx = av[0][1]
                elif pe_:
                    st = pe_[0][0]
                    idx = pe_[0][1]
                else:
                    continue
                if best is None or (st, idx) < (best[0], best[2]):
                    best = (st, e, idx)
            assert best is not None, "scheduler deadlock (cyclic deps?)"
            st, e, idx = best
            if avail[e] and avail[e][0][1] == idx:
                heapq.heappop(avail[e])
            else:
                heapq.heappop(pend[e])
            o = ops[idx]
            o.t0 = st
            t_free[e] = st + o.cost
            o.t1 = st + o.cost + o.lat
            order.append(o)
            done += 1
            for s_ in succ[idx]:
                extra = 0.0 if (s_.eng == o.eng and not o.dma) else sync_lat
                ready_t[s_.idx] = max(ready_t[s_.idx], o.t1 + extra)
                ndep[s_.idx] -= 1
                if ndep[s_.idx] == 0:
                    heapq.heappush(pend[s_.eng], (ready_t[s_.idx], s_.idx))
        self.ops = order
        for i, o in enumerate(order):
            o.idx = i
        self.sim_end = max(o.t1 for o in order)

    def finalize_plan(self, sem_alloc):
        for o in self.ops:
            for d in o.deps:
                if not self._nowait(o, d):
                    d.signal = True
        self.eng_sem = {}
        cnt = {e: 0 for e in ENGS}
        dcnt = {}
        dsems = {}
        for o in self.ops:
            if o.dma:
                key = o.sem
                if key not in dsems:
                    dsems[key] = sem_alloc(f"d_{key}")
                c = dcnt.get(key, 0) + 16 * o.dma
                dcnt[key] = c
                o.sem = dsems[key]
                o.val = c
            elif o.signal:
                if o.eng not in self.eng_sem:
                    self.eng_sem[o.eng] = sem_alloc(f"e_{o.eng}")
                cnt[o.eng] += 1
                o.sem = self.eng_sem[o.eng]
                o.val = cnt[o.eng]
        self.final_counts = cnt
        self.n_dsems = len(dsems)

    def run_engine(self, eng, e):
        waited = {}
        for o in self.ops:
            if o.eng != eng:
                continue
            need = {}
            for d in o.deps:
                if self._nowait(o, d):
                    continue
                k = id(d.sem)
                if k not in need or need[k][1] < d.val:
                    need[k] = (d.sem, d.val)
            for k, (sem, val) in need.items():
                if waited.get(k, 0) >= val:
                    continue
                e.wait_ge(sem, val)
                waited[k] = val
            r = o.fn(e)
            if o.dma:
                assert isinstance(r, (list, tuple)) and len(r) == o.dma, (o.name, r)
                for ins in r:
                    ins.then_inc(o.sem, 16)
            elif o.signal:
                assert r is not None, o.name
                r.then_inc(o.sem, 1)


D = 1024
NPT = 8
NST = 17
NSO = 16
NTILE_S = 32
NCTX = 2
NTOK = (NPT + NST) * 128
FFN = 2816
NJ = 22
NG = 11
EPS = 1e-6
SM_SCALE = 1.0 / math.sqrt(96.0)

RUN_SAMPLE = True
DEBUG = False
RESCHED = True
PRIO_BL = True
DVE_CONV_CHUNKS = ()


def pmaj(W):
    K, N = W.shape
    return np.ascontiguousarray(W.reshape(K // 128, 128, N).transpose(1, 0, 2))


def rope_tables(positions):
    half = 16
    inv_freq = (10000.0 ** (-np.arange(0, half, 2, dtype=np.float32) / half)).astype(np.float32)
    r = (positions // 64).astype(np.float32)
    c = (positions % 64).astype(np.float32)
    ang = np.concatenate([r[:, None] * inv_freq[None], c[:, None] * inv_freq[None]], axis=1).astype(np.float32)
    return np.cos(ang).astype(np.float32), np.sin(ang).astype(np.float32)


def const_tables():
    idx = np.arange(128)
    same = (idx[:, None] // 64) == (idx[None, :] // 64)
    c = -1.0 / 16.0
    trif = np.where(same & (idx[:, None] <= idx[None, :]), c, 0.0)
    trifs = np.where(same & (idx[:, None] > idx[None, :]), c, 0.0)
    trib = np.where(same & (idx[:, None] >= idx[None, :]), c, 0.0)
    tribs = np.where(same & (idx[:, None] < idx[None, :]), c, 0.0)
    maskf = np.where(same & (idx[:, None] <= idx[None, :]), 1.0, 0.0)
    maskb = np.where(same & (idx[:, None] >= idx[None, :]), 1.0, 0.0)
    ident = np.eye(128)
    return np.concatenate([ident, trif, trifs, trib, tribs, maskf, maskb], axis=1).astype(np.float32)


def prep_core(core, inp):
    b = core // 2
    rev = core % 2 == 1
    d = {}
    d["xp"] = np.ascontiguousarray(inp["x_prompt"][4 * core:4 * core + 4].reshape(1024, D))
    xs = inp["x_sample"][b]
    d["xs"] = np.ascontiguousarray(xs[::-1] if rev else xs)
    d["cckv"] = np.ascontiguousarray(inp["cache_mla_ckv"][b, 0])
    d["ckr"] = np.ascontiguousarray(inp["cache_mla_krope"][b, 0])
    st = inp["state_gla"][b, 0]
    d["st0"] = np.ascontiguousarray(st[::-1] if rev else st)
    cc = np.stack([inp["c_ctx"], inp["c"][b]], axis=1)
    d["cT"] = np.ascontiguousarray(cc.reshape(8, 128, 2).transpose(1, 0, 2))
    pos = np.arange(4096)
    if rev:
        pos = pos[::-1]
    cs, sn = rope_tables(pos)
    rt = np.stack([cs, sn], axis=1)
    d["rope"] = np.ascontiguousarray(rt.reshape(32, 128, 2, 16).transpose(1, 0, 2, 3))
    wal = np.zeros((2, 33, 2, 256), np.float32)
    for g in range(2):
        for ld in range(2):
            od = (1 - ld) if (g == 1 and rev) else ld
            wal[g, od * 16:(od + 1) * 16, ld, :] = inp["w_alpha_up"][0, od]
            wal[g, 32, ld, :] = inp["b_alpha"][0, od]
    d["wal"] = wal
    cw = inp["conv_w_dw"][0]
    cws = cw[::-1] if rev else cw
    cwt = np.stack([cw, cws], axis=0)
    d["cdw"] = np.ascontiguousarray(cwt.reshape(2, 31, 8, 128).transpose(3, 0, 2, 1))
    return d


def prep_shared(inp):
    d = {}
    w_in = inp["w_in"][0]
    o = np.cumsum([0, 384, 256, 32, 256, 256, 512, 512, 32])
    cq, ckv, kr, gq, gk, gv, gg, ga = [w_in[:, o[i]:o[i + 1]] for i in range(8)]
    d["w_mla"] = pmaj(np.concatenate([cq, ckv, kr], axis=1))
    d["w_gfm"] = pmaj(np.concatenate([gq, gk, gg, ga], axis=1))
    d["w_gtm"] = pmaj(np.concatenate([gk, gv], axis=1))
    d["w_uq"] = pmaj(inp["w_uq"][0])
    wukv = inp["w_ukv"][0].reshape(256, 8, 128)
    d["w_ukv"] = pmaj(np.concatenate([wukv[:, :, :64].reshape(256, 512), wukv[:, :, 64:].reshape(256, 512)], axis=1))
    d["w_out"] = pmaj(inp["w_out"][0])
    d["pw1"] = pmaj(inp["conv_w_pw1"][0])
    d["pw2"] = pmaj(inp["conv_w_pw2"][0])
    for l in range(2):
        d[f"wg{l}"] = np.ascontiguousarray(inp["ffn_w_gate"][l].reshape(8, 128, NG, 256).transpose(2, 1, 0, 3))
        d[f"wu{l}"] = np.ascontiguousarray(inp["ffn_w_up"][l].reshape(8, 128, NG, 256).transpose(2, 1, 0, 3))
        d[f"wd{l}"] = np.ascontiguousarray(inp["ffn_w_down"][l].reshape(NG, 2, 128, D).transpose(0, 2, 1, 3))
        d[f"wada{l}"] = np.ascontiguousarray(inp["w_ada"][l].reshape(8, 128, 12, 512).transpose(2, 1, 0, 3))
    d["bada"] = np.ascontiguousarray(inp["b_ada"])
    d["v_nm"] = np.ascontiguousarray(inp["norm_mix"])
    d["v_nf"] = np.ascontiguousarray(inp["norm_ffn"])
    qg = np.tile(inp["q_norm"][0], 8)
    kg = inp["k_norm"][0]
    vec = np.zeros((1, 8192), np.float32)
    def put(off, v):
        vec[0, off:off + v.size] = v.reshape(-1)
    put(0, inp["qa_norm"][0])
    put(384, inp["kva_norm"][0])
    put(640, qg)
    put(1408, np.tile(kg[:64], 8))
    put(1920, kg[64:])
    put(2048, inp["conv_b_pw2"][0])
    d["vec"] = vec
    fm = np.zeros((128, 64), np.float32)
    fm[:, 0] = inp["gla_norm"][0]
    fm[:, 1:9] = inp["conv_b_pw1"][0][:1024].reshape(8, 128).T
    fm[:, 9:17] = inp["conv_b_pw1"][0][1024:].reshape(8, 128).T
    fm[:, 17:25] = inp["conv_b_dw"][0].reshape(8, 128).T
    fm[:, 25:33] = inp["conv_ln_g"][0].reshape(8, 128).T
    fm[:, 33:41] = inp["conv_ln_b"][0].reshape(8, 128).T
    d["fm"] = fm
    d["cst"] = const_tables()
    return d


class PB:
    def __init__(self, nc, S, I, O):
        self.nc, self.S, self.I, self.O = nc, S, I, O
        self.outstores = []
        self.psA = S.psum_bufs[0:6]
        self.psB = S.psum_bufs[6:8]
        self.rrA = 0
        self.cvec = None

    def T(self, name, shape, dt=F32):
        return self.S.alloc(name, shape, dt)

    @staticmethod
    def _n(ap):
        return int(np.prod(ap.shape[1:]))

    def _ec(self, eng, ap):
        n = self._n(ap)
        if eng == "dve":
            return 0.1 + n / 1200.0
        if eng == "act":
            return 0.19 + n / 1900.0
        return 0.3 + n / 250.0

    def free(self, *b):
        self.S.free(*b)

    def ps(self):
        b = self.psA[self.rrA % len(self.psA)]
        self.rrA += 1
        return b

    def TT(self, eng, out, in0, in1, op, r, w):
        def fn(e):
            return e.tensor_tensor(out=out, in0=in0, in1=in1, op=op)
        return self.S.op(eng, fn, r, w, name="TT", cost=self._ec(eng, out))

    def STT(self, eng, out, in0, scalar, in1, op0, op1, r, w):
        def fn(e):
            return e.scalar_tensor_tensor(out=out, in0=in0, scalar=scalar, in1=in1, op0=op0, op1=op1)
        return self.S.op(eng, fn, r, w, name="STT", cost=self._ec(eng, out))

    def TS(self, eng, out, in0, s1, s2, op0, op1, r, w):
        def fn(e):
            if s2 is None:
                return e.tensor_scalar(out=out, in0=in0, scalar1=s1, scalar2=None, op0=op0)
            return e.tensor_scalar(out=out, in0=in0, scalar1=s1, scalar2=s2, op0=op0, op1=op1)
        return self.S.op(eng, fn, r, w, name="TS", cost=self._ec(eng, out))

    def CP(self, eng, out, in_, r, w):
        def fn(e):
            if eng == "act":
                return e.copy(out=out, in_=in_)
            return e.tensor_copy(out=out, in_=in_)
        return self.S.op(eng, fn, r, w, name="CP", cost=self._ec(eng, out))

    def ACT(self, out, in_, func, r, w, bias=None, scale=1.0, accum=None):
        def fn(e):
            kw = {}
            if bias is not None:
                kw["bias"] = bias
            if accum is not None:
                kw["accum_out"] = accum
            return e.activation(out=out, in_=in_, func=func, scale=scale, **kw)
        return self.S.op("act", fn, r, w, name="ACT", cost=self._ec("act", out))

    def MEMSET(self, eng, ap, val, w):
        def fn(e):
            return e.memset(ap, val)
        return self.S.op(eng, fn, (), w, name="MEMSET", cost=self._ec(eng, ap))

    def RED(self, out, in_, r, w):
        def fn(e):
            return e.tensor_reduce(out=out, in_=in_, axis=AX.X, op=ALU.add)
        return self.S.op("dve", fn, r, w, name="RED", cost=self._ec("dve", in_))

    def RECIP(self, out, in_, r, w):
        def fn(e):
            return e.reciprocal(out=out, in_=in_)
        return self.S.op("dve", fn, r, w, name="RECIP", cost=self._ec("dve", out) * 2)

    def load(self, dst, src_ap, eng="sp", r=(), dst_ap=None, name="ld"):
        da = dst.ap if dst_ap is None else dst_ap
        def fn(e):
            return [e.dma_start(out=da, in_=src_ap)]
        nb = self._n(da) * 128 * (4 if da.dtype == F32 else 2)
        return self.S.op(eng, fn, reads=list(r), writes=[dst], dma=1, name=name, cost=(0.15 if eng == "sp" else 1.0), lat=2.0 + nb / 2.0e5)

    def store(self, dst_ap, src, eng="sp", w=(), src_ap=None, final=False, name="st"):
        sa = src.ap if src_ap is None else src_ap
        def fn(e):
            return [e.dma_start(out=dst_ap, in_=sa)]
        nb = self._n(sa) * 128 * (4 if sa.dtype == F32 else 2)
        o = self.S.op(eng, fn, reads=[src], writes=list(w), dma=1, name=name, cost=(0.15 if eng == "sp" else 1.0), lat=2.0 + nb / 2.0e5)
        if final:
            self.outstores.append(o)
        return o

    def mm(self, psbs, groups, r, name="mm"):
        if not isinstance(psbs, (list, tuple)):
            psbs = [psbs]
        groups = [g if len(g) == 4 else (g[0], g[1], True, True) for g in groups]
        def fn(e):
            last = None
            for (pa, pairs, st, sp_) in groups:
                n = len(pairs)
                for i, (l, rh) in enumerate(pairs):
                    last = e.matmul(pa, lhsT=l, rhs=rh, start=(st and i == 0), stop=(sp_ and i == n - 1))
            return last
        c = 0.0
        for (pa, pairs, st, sp_) in groups:
            for (l, rh) in pairs:
                c += max(64, self._n(rh)) / 2200.0 + 0.02
        return self.S.op("pe", fn, r, list(psbs), name=name, cost=c, lat=0.1)

    def tr(self, psb, items, r, ident, name="tr"):
        items = list(items)
        def fn(e):
            last = None
            for (pa, ia) in items:
                last = e.transpose(out=pa, in_=ia, identity=ident)
            return last
        return self.S.op("pe", fn, r, [psb], name=name, cost=0.1 * len(items), lat=0.1)

    def rstd(self, ss_buf, ss_ap, out_buf, out_ap, inv_n, tmp_buf, tmp_ap):
        npart = ss_ap.shape[0]
        self.ACT(tmp_ap, ss_ap, AF.Ln, [ss_buf, self.cvec], [tmp_buf], bias=self.cvec.ap[0:npart, 0:1], scale=inv_n)
        self.ACT(out_ap, tmp_ap, AF.Exp, [tmp_buf], [out_buf], scale=-0.5)


def build_program(nc, S, I, O):
    P = PB(nc, S, I, O)

    def dram(name, shape, dt=F32):
        ap = nc.dram_tensor(name, list(shape), dt, kind=("ExternalOutput" if DEBUG else "Internal")).ap()
        return Buf(name, ap, is_dram=True)
    MODa = [dram(f"MODa{l}", [2, 2 * D]) for l in range(2)]
    MODb = [dram(f"MODb{l}", [2, 4 * D]) for l in range(2)]
    X2 = dram("X2", [NTOK, D])
    UT = dram("UT", [D, NTOK])
    ATT = dram("ATT", [NTOK, 512], BF16)
    GLT = dram("GLT", [4, 128, NTOK], BF16)
    HTS = dram("HTS", [NPT + NTILE_S, 128, 8, 128], BF16)
    WG = [dram(f"WGs{l}", [NG, 128, 8, 256], BF16) for l in range(2)]
    WU = [dram(f"WUs{l}", [NG, 128, 8, 256], BF16) for l in range(2)]
    WD = [dram(f"WDs{l}", [NG, 128, 2, 1024], BF16) for l in range(2)]

    def precast_ffn():
        stg = [P.T(f"stg{i}", [128, 2048], BF16) for i in range(4)]
        work = []
        for l in range(2):
            for gi in range(NG):
                work.append((I[f"wg{l}"][gi].rearrange("p a b -> p (a b)"), WG[l], WG[l].ap[gi].rearrange("p a b -> p (a b)")))
                work.append((I[f"wu{l}"][gi].rearrange("p a b -> p (a b)"), WU[l], WU[l].ap[gi].rearrange("p a b -> p (a b)")))
                work.append((I[f"wd{l}"][gi].rearrange("p a b -> p (a b)"), WD[l], WD[l].ap[gi].rearrange("p a b -> p (a b)")))
        LEAD = 3
        n = len(work)
        for i in range(n + LEAD):
            if i < n:
                P.load(stg[i % 4], work[i][0], eng="pool", name="precast_ld")
            k = i - LEAD
            if k >= 0:
                P.store(work[k][2], stg[k % 4], eng="pool", w=[work[k][1]], name="precast_st")
        return stg

    cst = P.T("cst", [128, 7 * 128])
    P.load(cst, I["cst"])
    ident_f = cst.ap[:, 0:128]
    TRI = {("f", 0): cst.ap[:, 128:256], ("f", 1): cst.ap[:, 256:384], ("b", 0): cst.ap[:, 384:512], ("b", 1): cst.ap[:, 512:640]}
    MASK = {"f": cst.ap[:, 640:768], "b": cst.ap[:, 768:896]}
    identb = P.T("identb", [128, 128], BF16)
    P.CP("dve", identb.ap, ident_f, [cst], [identb])
    onesb = P.T("onesb", [128, 128], BF16)
    P.MEMSET("pool", onesb.ap, 1.0, [onesb])
    cvec = P.T("cvec", [128, 4])
    P.cvec = cvec
    P.MEMSET("pool", cvec.ap[:, 0:1], EPS, [cvec])
    P.MEMSET("pool", cvec.ap[:, 2:3], 1.0, [cvec])
    fm = P.T("fm", [128, 64])
    P.load(fm, I["fm"])
    rope = P.T("rope", [128, 32, 2, 16])
    P.load(rope, I["rope"])

    def ada_layer(l, part):
        pcs = range(0, 4) if part == 0 else range(4, 12)
        ncol = len(pcs) * 512
        c0 = pcs[0] * 512
        cT = P.T("cT", [128, 8, 2])
        P.load(cT, I["cT"])
        scb = P.T("scb", [128, 8, 2], BF16)
        P.ACT(scb.ap, cT.ap, AF.Silu, [cT], [scb])
        bad = P.T("bad", [2, ncol])
        P.load(bad, I["bada"][l:l + 1, c0:c0 + ncol].partition_broadcast(2))
        mo = P.T("mo", [2, ncol])
        wp = [P.T(f"wada{i}", [128, 8, 512], BF16) for i in range(2)]
        for i, pc in enumerate(pcs):
            w = wp[i % 2]
            P.load(w, I[f"wada{l}"][pc], eng="pool")
            psb = P.ps()
            P.mm(psb, [(psb.ap[0:2, :], [(scb.ap[:, k, :], w.ap[:, k, :]) for k in range(8)])], r=[scb, w])
            P.TT("dve", mo.ap[:, i * 512:(i + 1) * 512], psb.ap[0:2, :], bad.ap[:, i * 512:(i + 1) * 512], ALU.add, [psb, bad], [mo])
        dstb = (MODa if part == 0 else MODb)[l]
        P.store(dstb.ap, mo, w=[dstb])
        P.free(cT, scb, bad, mo, *wp)

    def load_mod(l, g, which, name):
        out = {}
        for k in which:
            t = P.T(f"mod{name}{k}", [128, D])
            if k < 2:
                src, kk = MODa[l], k
            else:
                src, kk = MODb[l], k - 2
            P.load(t, src.ap[g:g + 1, kk * D:(kk + 1) * D].partition_broadcast(128), r=[src])
            out[k] = t
        return out

    def make_A(scbuf, gain_row_ap, name):
        gt = P.T(name + "_g", [128, D])
        P.load(gt, gain_row_ap.partition_broadcast(128))
        P.STT("dve", scbuf.ap, scbuf.ap, 1.0, gt.ap, ALU.add, ALU.mult, [scbuf, gt], [scbuf])
        P.free(gt)

    def mk_wk(pfx):
        return dict(ss=P.T(pfx + "ss", [128, 1]), t1=P.T(pfx + "t1", [128, 1]), junk=P.T(pfx + "junk", [128, D], BF16),
                    tmp=P.T(pfx + "tmp", [128, D]), hb=P.T(pfx + "hb", [128, D], BF16))

    def free_wk(wk):
        P.free(*wk.values())

    def tm_norm_mod(xt, A, Bm, hT_buf, hT_ap, wk):
        ss, t1, junk, tmp, hb = wk["ss"], wk["t1"], wk["junk"], wk["tmp"], wk["hb"]
        P.ACT(junk.ap, xt.ap, AF.Square, [xt], [junk, ss], accum=ss.ap)
        P.rstd(ss, ss.ap, ss, ss.ap, 1.0 / D, t1, t1.ap)
        P.STT("dve", tmp.ap, xt.ap, ss.ap, A.ap, ALU.mult, ALU.mult, [xt, ss, A], [tmp])
        P.TT("dve", hb.ap, tmp.ap, Bm.ap, ALU.add, [tmp, Bm], [hb])
        psb = P.ps()
        pv = psb.ap.bitcast(BF16).rearrange("p (c t) -> p c t", c=8)
        P.tr(psb, [(pv[:, c, :], hb.ap[:, c * 128:(c + 1) * 128]) for c in range(8)], r=[hb, identb], ident=identb.ap)
        P.CP("act", hT_ap, pv, [psb], [hT_buf])

    def rope_apply(src_buf, src_ap, dst_buf, dst_ap, tile_idx, nh, tb1, tb2):
        cs = rope.ap[:, tile_idx, 0, :].rearrange("p (a f) -> p a f", a=2)
        sn = rope.ap[:, tile_idx, 1, :].rearrange("p (a f) -> p a f", a=2)
        s5 = src_ap.rearrange("p h (a b f) -> p h a b f", a=2, b=2)
        d5 = dst_ap.rearrange("p h (a b f) -> p h a b f", a=2, b=2)
        t5 = tb1.ap.rearrange("p (h a b f) -> p h a b f", h=nh, a=2, b=2)
        x5 = tb2.ap.rearrange("p (h a b f) -> p h a b f", h=nh, a=2, b=2)
        csb = cs.unsqueeze(1).to_broadcast([128, nh, 2, 8])
        snb = sn.unsqueeze(1).to_broadcast([128, nh, 2, 8])
        P.TT("dve", t5[:, :, :, 0, :], s5[:, :, :, 1, :], snb, ALU.mult, [src_buf, rope], [tb1])
        P.TT("dve", t5[:, :, :, 1, :], s5[:, :, :, 0, :], snb, ALU.mult, [src_buf, rope, tb1], [tb1])
        P.TT("dve", x5[:, :, :, 0, :], s5[:, :, :, 0, :], csb, ALU.mult, [src_buf, rope], [tb2])
        P.TT("dve", x5[:, :, :, 1, :], s5[:, :, :, 1, :], csb, ALU.mult, [src_buf, rope, tb2], [tb2])
        P.TT("dve", d5[:, :, :, 0, :], x5[:, :, :, 0, :], t5[:, :, :, 0, :], ALU.subtract, [tb1, tb2], [dst_buf])
        P.TT("dve", d5[:, :, :, 1, :], x5[:, :, :, 1, :], t5[:, :, :, 1, :], ALU.add, [tb1, tb2, dst_buf], [dst_buf])

    jobs = []
    for s in range(4):
        jobs.append(dict(name=f"p{s}", g=0, xsrc=I["xp"], ext=[2 * s, 2 * s + 1], other=[], ctx=False, rope=False,
                         row0=2 * s * 128, seq=s))
    if RUN_SAMPLE:
        jobs.append(dict(name="s", g=1, xsrc=I["xs"], ext=list(range(NST)), other=list(range(NST, NTILE_S)), ctx=True, rope=True,
                         row0=NPT * 128, seq=None))

    S.phase = 'ada0'
    ada_layer(0, 0)
    ada_layer(0, 1)

    def mla_pass():
        w_mla = P.T("w_mla", [128, 8, 672], BF16)
        P.load(w_mla, I["w_mla"], eng="pool")
        w_uq = P.T("w_uq", [128, 3, 768], BF16)
        P.load(w_uq, I["w_uq"], eng="pool")
        w_ukv = P.T("w_ukv", [128, 2, 1024], BF16)
        P.load(w_ukv, I["w_ukv"], eng="pool")
        vecm = P.T("vecm", [128, 1952])
        P.load(vecm, I["vec"][0:1, 0:1952].partition_broadcast(128))
        qa_bc = vecm.ap[:, 0:384]
        kva_bc = vecm.ap[:, 384:640]
        qg_bc = vecm.ap[:, 640:1408]
        kgn_bc = vecm.ap[:, 1408:1920]
        kgr_bc = vecm.ap[:, 1920:1952]
        NKMAX = (NCTX + NTILE_S) if RUN_SAMPLE else 2
        KT = P.T("KT", [96, 8, NKMAX * 128], BF16)
        VA = P.T("VA", [128, NKMAX, 8, 65], BF16)
        P.MEMSET("pool", VA.ap[:, :, :, 64:65], 1.0, [VA])
        NQMAX = NST if RUN_SAMPLE else 2
        cqT = P.T("cqT", [128, 3, NQMAX * 128], BF16)
        stg_keep = []

        def build_keys(job):
            g = job["g"]
            m = load_mod(0, g, [0, 1], "")
            make_A(m[1], I["v_nm"][0:1, :], "mod")
            A_, B_ = m[1], m[0]
            wks = [mk_wk(f"m{i_}_") for i_ in range(2)]
            xts_ = [P.T(f"m_x{i_}", [128, D]) for i_ in range(2)]
            hTs_ = [P.T(f"m_hT{i_}", [128, 8, 128], BF16) for i_ in range(2)]
            kcnt = [0]

            class BS_:
                pass
            bsets = []
            allb = [A_, B_, *xts_, *hTs_]
            for i_ in range(2):
                b_ = BS_()
                b_.ssq, b_.ssk, b_.ssr, b_.t1 = (P.T(f"m{i_}_{n}", [128, 1]) for n in ("ssq", "ssk", "ssr", "t1b"))
                b_.ssn, b_.rs, b_.t8 = (P.T(f"m{i_}_{n}", [128, 8]) for n in ("ssn", "rs", "t8"))
                b_.cqn = P.T(f"m{i_}_cqn", [128, 384], BF16)
                b_.ckvn = P.T(f"m{i_}_ckvn", [128, 256])
                b_.ckvb = P.T(f"m{i_}_ckvb", [128, 256], BF16)
                b_.ckT = P.T(f"m{i_}_ckT", [128, 2, 128], BF16)
                b_.krr = P.T(f"m{i_}_krr", [128, 32])
                b_.krg = P.T(f"m{i_}_krg", [128, 32])
                b_.krt1 = P.T(f"m{i_}_krt1", [128, 32])
                b_.krt2 = P.T(f"m{i_}_krt2", [128, 32])
                b_.kro = P.T(f"m{i_}_kro", [128, 32])
                b_.krj = P.T(f"m{i_}_krj", [128, 32])
                b_.ktm = P.T(f"m{i_}_ktm", [128, 8, 96], BF16)
                b_.sq5 = P.T(f"m{i_}_sq5", [128, 512])
                b_.kn2 = P.T(f"m{i_}_kn2", [128, 512])
                allb += [b_.ssq, b_.ssk, b_.ssr, b_.t1, b_.ssn, b_.rs, b_.t8, b_.cqn, b_.ckvn, b_.ckvb, b_.ckT, b_.krr, b_.krg, b_.krt1,
                         b_.krt2, b_.kro, b_.krj, b_.ktm, b_.sq5, b_.kn2]
                bsets.append(b_)
            cur = [bsets[0], wks[0]]

            def kv_from_latent(ktile, ck_buf, ck_ap, kr_buf, kr_ap, rope_tile):
                b_ = cur[0]
                ssr, ssn, rs, t8, ckvb, ckT, krg, krt1, krt2, kro, krj, ktm, sq5, kn2 = (b_.ssr, b_.ssn, b_.rs, b_.t8, b_.ckvb, b_.ckT, b_.krg,
                                                                                      b_.krt1, b_.krt2, b_.kro, b_.krj, b_.ktm, b_.sq5, b_.kn2)
                P.CP("dve", ckvb.ap, ck_ap, [ck_buf], [ckvb])
                psb = P.ps()
                pv = psb.ap.bitcast(BF16)[:, 0:256].rearrange("p (c t) -> p c t", c=2)
                P.tr(psb, [(pv[:, c, :], ckvb.ap[:, c * 128:(c + 1) * 128]) for c in range(2)], r=[ckvb, identb], ident=identb.ap)
                P.CP("act", ckT.ap, pv, [psb], [ckT])
                psk, psv = P.ps(), P.ps()
                P.mm(psk, [(psk.ap, [(ckT.ap[:, c, :], w_ukv.ap[:, c, 0:512]) for c in range(2)])], r=[ckT, w_ukv])
                P.mm(psv, [(psv.ap, [(ckT.ap[:, c, :], w_ukv.ap[:, c, 512:1024]) for c in range(2)])], r=[ckT, w_ukv])
                P.CP("act", VA.ap[:, ktile, :, 0:64], psv.ap.rearrange("p (h d) -> p h d", h=8), [psv], [VA])
                P.ACT(sq5.ap, psk.ap, AF.Square, [psk], [sq5])
                P.RED(ssn.ap, sq5.ap.rearrange("p (h d) -> p h d", h=8), [sq5], [ssn])
                P.ACT(krj.ap, kr_ap, AF.Square, [kr_buf], [krj, ssr], accum=ssr.ap)
                P.TS("dve", ssn.ap, ssn.ap, ssr.ap, None, ALU.add, None, [ssn, ssr], [ssn])
                P.rstd(ssn, ssn.ap, rs, rs.ap, 1.0 / 96.0, t8, t8.ap)
                P.TT("dve", kn2.ap, psk.ap, kgn_bc, ALU.mult, [psk, vecm], [kn2])
                P.TT("dve", ktm.ap[:, :, 0:64], kn2.ap.rearrange("p (h d) -> p h d", h=8),
                     rs.ap.unsqueeze(2).to_broadcast([128, 8, 64]), ALU.mult, [kn2, rs], [ktm])
                P.TT("dve", krg.ap, kr_ap, kgr_bc, ALU.mult, [kr_buf, vecm], [krg])
                if rope_tile is not None:
                    rope_apply(krg, krg.ap.unsqueeze(1), kro, kro.ap.unsqueeze(1), rope_tile, 1, krt1, krt2)
                    ksrc = kro
                else:
                    ksrc = krg
                P.TT("dve", ktm.ap[:, :, 64:96], ksrc.ap.unsqueeze(1).to_broadcast([128, 8, 32]),
                     rs.ap.unsqueeze(2).to_broadcast([128, 8, 32]), ALU.mult, [ksrc, rs], [ktm])
                pst = P.ps()
                ptv = pst.ap.bitcast(BF16).rearrange("p (h t) -> p h t", h=8)
                P.tr(pst, [(ptv[0:96, h, :], ktm.ap[:, h, :]) for h in range(8)], r=[ktm, identb], ident=identb.ap)
                P.CP("act", KT.ap[:, :, ktile * 128:(ktile + 1) * 128], ptv[0:96], [pst], [KT])

            def key_tile(t, kidx, full):
                xt = xts_[kcnt[0] % 2]
                hT = hTs_[kcnt[0] % 2]
                cur[0] = bsets[kcnt[0] % 2]
                cur[1] = wks[kcnt[0] % 2]
                kcnt[0] += 1
                b_ = cur[0]
                wk = cur[1]
                ssq, ssk, t1, cqn, ckvn, krr = b_.ssq, b_.ssk, b_.t1, b_.cqn, b_.ckvn, b_.krr
                P.load(xt, job["xsrc"][t * 128:(t + 1) * 128, :])
                tm_norm_mod(xt, A_, B_, hT, hT.ap, wk)
                P.store(HTS.ap[t if g == 0 else NPT + t], hT, eng="sp", w=[HTS])
                psB_ = P.ps()
                P.mm(psB_, [(psB_.ap[:, 0:288], [(hT.ap[:, c, :], w_mla.ap[:, c, 384:672]) for c in range(8)])], r=[hT, w_mla])
                if full:
                    psA_ = P.ps()
                    P.mm(psA_, [(psA_.ap[:, 0:384], [(hT.ap[:, c, :], w_mla.ap[:, c, 0:384]) for c in range(8)])], r=[hT, w_mla])
                    P.ACT(wk["junk"].ap[:, 0:384], psA_.ap[:, 0:384], AF.Square, [psA_], [wk["junk"], ssq], accum=ssq.ap)
                    P.rstd(ssq, ssq.ap, ssq, ssq.ap, 1.0 / 384.0, t1, t1.ap)
                    P.STT("dve", cqn.ap, psA_.ap[:, 0:384], ssq.ap, qa_bc, ALU.mult, ALU.mult, [psA_, ssq, vecm], [cqn])
                    pst = P.ps()
                    pv = pst.ap.bitcast(BF16)[:, 0:384].rearrange("p (c t) -> p c t", c=3)
                    P.tr(pst, [(pv[:, c, :], cqn.ap[:, c * 128:(c + 1) * 128]) for c in range(3)], r=[cqn, identb], ident=identb.ap)
                    qi = job["ext"].index(t)
                    P.CP("act", cqT.ap[:, :, qi * 128:(qi + 1) * 128], pv, [pst], [cqT])
                P.ACT(wk["junk"].ap[:, 0:256], psB_.ap[:, 0:256], AF.Square, [psB_], [wk["junk"], ssk], accum=ssk.ap)
                P.rstd(ssk, ssk.ap, ssk, ssk.ap, 1.0 / 256.0, t1, t1.ap)
                P.STT("dve", ckvn.ap, psB_.ap[:, 0:256], ssk.ap, kva_bc, ALU.mult, ALU.mult, [psB_, ssk, vecm], [ckvn])
                P.CP("act", krr.ap, psB_.ap[:, 256:288], [psB_], [krr])
                if g == 0:
                    row = t * 128
                    P.store(O["ckv_o"][row:row + 128, :], ckvn, final=True)
                    P.store(O["kr_o"][row:row + 128, :], krr, final=True)
                kv_from_latent(kidx, ckvn, ckvn.ap, krr, krr.ap, t if job["rope"] else None)

            kidx = 0
            if job["ctx"]:
                for ct in range(NCTX):
                    cl = P.T("c_lat", [128, 256])
                    ck = P.T("c_kr", [128, 32])
                    P.load(cl, I["cckv"][ct * 128:(ct + 1) * 128, :])
                    P.load(ck, I["ckr"][ct * 128:(ct + 1) * 128, :])
                    cur[0] = bsets[ct % 2]
                    kv_from_latent(kidx, cl, cl.ap, ck, ck.ap, None)
                    P.free(cl, ck)
                    kidx += 1
            for t in job["ext"]:
                key_tile(t, kidx, True)
                kidx += 1
            for t in job["other"]:
                key_tile(t, kidx, False)
                kidx += 1
            for wk_ in wks:
                free_wk(wk_)
            P.free(*allb)
            return kidx

        def pv_op(PT, kc, ov, h, nt, nk, pso):
            def fn(e):
                last = None
                for j in range(nt):
                    last = e.matmul(ov[:, j, :], lhsT=PT.ap[:, j * 128:(j + 1) * 128], rhs=VA.ap[:, kc, h, :],
                                    start=(kc == 0 and j == 0), stop=(kc == nk - 1), skip_group_check=True)
                return last
            P.S.op("pe", fn, [PT, VA], [pso], name="pv", cost=0.07 * nt, lat=0.1)

        def attention(job, nk):
            ext = job["ext"]
            nq = len(ext)
            qsb = P.T("a_qsb", [128, 768])
            qsq = P.T("a_qsq", [128, 768])
            qn = P.T("a_qn", [128, 8, 96])
            qrp = P.T("a_qrp", [128, 8, 32])
            qrt1 = P.T("a_qrt1", [128, 256])
            qrt2 = P.T("a_qrt2", [128, 256])
            qbf = P.T("a_qbf", [128, 8, 96], BF16)
            ssq8 = P.T("a_ssq8", [128, 8])
            rq8 = P.T("a_rq8", [128, 8])
            t8 = P.T("a_t8", [128, 8])
            qTs = [P.T(f"a_qT{i}", [96, 8, 512], BF16) for i in range(2)]
            PTs = [P.T(f"a_PT{i}", [128, 512], BF16) for i in range(6)]
            rc = P.T("a_rc", [128, 4])
            aos = [P.T(f"a_ao{i}", [128, 4, 512], BF16) for i in range(2)]
            pcount = 0
            for bi, b0 in enumerate(range(0, nq, 4)):
                tiles = ext[b0:b0 + 4]
                nt = len(tiles)
                TB = nt * 128
                qT = qTs[bi % 2]
                ao = aos[bi % 2]
                for j, t in enumerate(tiles):
                    qi = b0 + j
                    ps1, ps2 = P.ps(), P.ps()
                    P.mm(ps1, [(ps1.ap[:, 0:480], [(cqT.ap[:, c, qi * 128:(qi + 1) * 128], w_uq.ap[:, c, 0:480]) for c in range(3)])],
                         r=[cqT, w_uq])
                    P.mm(ps2, [(ps2.ap[:, 0:288], [(cqT.ap[:, c, qi * 128:(qi + 1) * 128], w_uq.ap[:, c, 480:768]) for c in range(3)])],
                         r=[cqT, w_uq])
                    P.CP("act", qsb.ap[:, 0:480], ps1.ap[:, 0:480], [ps1], [qsb])
                    P.CP("act", qsb.ap[:, 480:768], ps2.ap[:, 0:288], [ps2, qsb], [qsb])
                    P.TT("dve", qsq.ap, qsb.ap, qsb.ap, ALU.mult, [qsb], [qsq])
                    P.RED(ssq8.ap, qsq.ap.rearrange("p (h d) -> p h d", h=8), [qsq], [ssq8])
                    P.rstd(ssq8, ssq8.ap, rq8, rq8.ap, 1.0 / 96.0, t8, t8.ap)
                    P.TT("dve", qsq.ap, qsb.ap, qg_bc, ALU.mult, [qsb, vecm, qsq], [qsq])
                    P.TT("dve", qn.ap, qsq.ap.rearrange("p (h d) -> p h d", h=8), rq8.ap.unsqueeze(2).to_broadcast([128, 8, 96]),
                         ALU.mult, [qsq, rq8], [qn])
                    if job["rope"]:
                        rope_apply(qn, qn.ap[:, :, 64:96], qrp, qrp.ap, t, 8, qrt1, qrt2)
                        P.CP("dve", qbf.ap[:, :, 0:64], qn.ap[:, :, 0:64], [qn], [qbf])
                        P.CP("dve", qbf.ap[:, :, 64:96], qrp.ap, [qrp, qbf], [qbf])
                    else:
                        P.CP("dve", qbf.ap, qn.ap, [qn], [qbf])
                    pst = P.ps()
                    ptv = pst.ap.bitcast(BF16).rearrange("p (h t) -> p h t", h=8)
                    P.tr(pst, [(ptv[0:96, h, :], qbf.ap[:, h, :]) for h in range(8)], r=[qbf, identb], ident=identb.ap)
                    P.CP("act", qT.ap[:, :, j * 128:(j + 1) * 128], ptv[0:96], [pst], [qT])
                items = [(h, kc) for h in range(8) for kc in range(nk)]
                nit = len(items)
                pss_l = [None] * nit

                def qk(i):
                    h, kc = items[i]
                    pss = P.ps()
                    P.mm(pss, [(pss.ap[:, 0:TB], [(KT.ap[:, h, kc * 128:(kc + 1) * 128], qT.ap[:, h, 0:TB])])], r=[KT, qT])
                    pss_l[i] = pss
                qk(0)
                for i in range(nit):
                    h, kc = items[i]
                    pso = P.psB[h % 2]
                    ov = pso.ap[:, 0:nt * 65].rearrange("p (t d) -> p t d", t=nt)
                    pss = pss_l[i]
                    PT = PTs[pcount % 6]
                    pcount += 1
                    P.ACT(PT.ap[:, 0:TB], pss.ap[:, 0:TB], AF.Exp, [pss], [PT], scale=SM_SCALE)
                    if i + 1 < nit:
                        qk(i + 1)
                    pv_op(PT, kc, ov, h, nt, nk, pso)
                    if kc == nk - 1:
                        P.RECIP(rc.ap[:, 0:nt], ov[:, :, 64], [pso], [rc])
                        P.TT("dve", ao.ap[:, 0:nt, h * 64:(h + 1) * 64], ov[:, :, 0:64],
                             rc.ap[:, 0:nt].unsqueeze(2).to_broadcast([128, nt, 64]), ALU.mult, [pso, rc], [ao])
                row = job["row0"] + b0 * 128
                P.store(ATT.ap[row:row + TB, :].rearrange("(t p) f -> p t f", p=128), ao, eng="sp", src_ap=ao.ap[:, 0:nt, :], w=[ATT])
            P.free(qsb, qsq, qn, qrp, qrt1, qrt2, qbf, ssq8, rq8, t8, rc, *qTs, *PTs, *aos)

        for job in jobs:
            S.phase = 'mla_build_' + job["name"]
            nk = build_keys(job)
            if job["g"] == 1 or not RUN_SAMPLE and job["seq"] == 3:
                P.free(w_mla, w_ukv)
                stg_keep.extend(precast_ffn())
            S.phase = 'mla_attn_' + job["name"]
            attention(job, nk)
        P.free(*stg_keep)
        P.free(w_uq, vecm, KT, VA, cqT)

    S.phase = 'mla'
    mla_pass()

    def gla_pass():
        w_gfm = P.T("w_gfm", [128, 8, 1056], BF16)
        P.load(w_gfm, I["w_gfm"], eng="pool")
        w_gtm = P.T("w_gtm", [128, 8, 768], BF16)
        P.load(w_gtm, I["w_gtm"], eng="pool")
        walb = P.T("walb", [33, 2, 2, 256])
        P.load(walb, I["wal"].rearrange("g k d n -> k g d n"))
        gn_ap = fm.ap[:, 0:1]
        bsets_g = []
        for i_ in range(2):
            hT_ = P.T(f"g_hT{i_}", [128, 8, 512], BF16)
            gqT_ = P.T(f"g_gqT{i_}", [64, 4, 512])
            gkT_ = P.T(f"g_gkT{i_}", [64, 4, 512])
            gaT_ = P.T(f"g_gaT{i_}", [33, 512])
            P.MEMSET("dve", gaT_.ap[32:33, :], 1.0, [gaT_])
            bsets_g.append((hT_, gqT_, gkT_, gaT_))
        bcur = [0]
        sgg = P.T("g_sgg", [128, 4, 512])
        class TS_:
            pass
        tsets = []
        for i_ in range(2):
            t_ = TS_()
            t_.gktm = P.T(f"g_gktm{i_}", [128, 256])
            t_.gvb = P.T(f"g_gvb{i_}", [128, 512], BF16)
            t_.ez = P.T(f"g_ez{i_}", [128, 256])
            t_.lz = P.T(f"g_lz{i_}", [128, 256])
            t_.E = P.T(f"g_E{i_}", [64, 4, 128])
            t_.En = P.T(f"g_En{i_}", [64, 4, 128])
            t_.Es = P.T(f"g_Es{i_}", [128, 256])
            t_.qt = P.T(f"g_qt{i_}", [64, 4, 128], BF16)
            t_.kt = P.T(f"g_kt{i_}", [64, 4, 128], BF16)
            t_.kh = P.T(f"g_kh{i_}", [128, 256], BF16)
            t_.Ab = P.T(f"g_Ab{i_}", [128, 4, 128], BF16)
            tsets.append(t_)
        tcnt = [0]
        Sst = P.T("g_S", [64, 4, 128])
        Sb = P.T("g_Sb", [64, 4, 128], BF16)
        St = P.T("g_St", [64, 4, 128])
        NE = NST if RUN_SAMPLE else 2
        ost = P.T("g_ost", [128, 4, NE * 128])
        osq = P.T("g_osq", [128, 512], BF16)
        rsd = P.T("g_rsd", [128, 512])
        rt1 = P.T("g_rt1", [128, 512])
        gl = P.T("g_gl", [128, 4, 512], BF16)
        pso = P.psB[0]
        pov = pso.ap.rearrange("p (h t) -> p h t", h=4)

        def block_prep(job, A_, B_, tiles, d, full):
            bcur[0] += 1
            hT, gqT, gkT, gaT = bsets_g[bcur[0] % 2]
            TB = len(tiles) * 128
            for j, t in enumerate(tiles):
                P.load(hT, HTS.ap[t if job["g"] == 0 else NPT + t], r=[HTS], dst_ap=hT.ap[:, :, j * 128:(j + 1) * 128])
            if full:
                for (dst, off) in ((gqT, 0), (gkT, 256)):
                    for h in range(4):
                        psb = P.ps()
                        P.mm(psb, [(psb.ap[0:64, 0:TB], [(w_gfm.ap[:, c, off + h * 64: off + (h + 1) * 64], hT.ap[:, c, 0:TB])
                                                         for c in range(8)])], r=[w_gfm, hT])
                        P.CP("act", dst.ap[:, h, 0:TB], psb.ap[0:64, 0:TB], [psb], [dst])
            psb = P.ps()
            P.mm(psb, [(psb.ap[0:32, 0:TB], [(w_gfm.ap[:, c, 1024:1056], hT.ap[:, c, 0:TB]) for c in range(8)])], r=[w_gfm, hT])
            P.CP("dve", gaT.ap[0:32, 0:TB], psb.ap[0:32, 0:TB], [psb], [gaT])
            if full and d == "f":
                for h in range(4):
                    psb = P.ps()
                    P.mm(psb, [(psb.ap[:, 0:TB], [(w_gfm.ap[:, c, 512 + h * 128: 512 + (h + 1) * 128], hT.ap[:, c, 0:TB])
                                                  for c in range(8)])], r=[w_gfm, hT])
                    P.ACT(sgg.ap[:, h, 0:TB], psb.ap[:, 0:TB], AF.Silu, [psb], [sgg])

        def tile_prep(job, j, d, full):
            tcnt[0] += 1
            ts = tsets[tcnt[0] % 2]
            gktm, gvb, ez, lz, E, En, Es, qt, kt, kh = ts.gktm, ts.gvb, ts.ez, ts.lz, ts.E, ts.En, ts.Es, ts.qt, ts.kt, ts.kh
            hT, gqT, gkT, gaT = bsets_g[bcur[0] % 2]
            g = job["g"]
            di = 0 if d == "f" else 1
            c0 = j * 128
            psk, psv = P.ps(), P.ps()
            P.mm(psk, [(psk.ap[:, 0:256], [(hT.ap[:, c, c0:c0 + 128], w_gtm.ap[:, c, 0:256]) for c in range(8)])], r=[hT, w_gtm])
            P.mm(psv, [(psv.ap, [(hT.ap[:, c, c0:c0 + 128], w_gtm.ap[:, c, 256:768]) for c in range(8)])], r=[hT, w_gtm])
            P.CP("dve", gktm.ap, psk.ap[:, 0:256], [psk], [gktm])
            P.CP("act", gvb.ap, psv.ap, [psv], [gvb])
            psz = P.ps()
            P.mm(psz, [(psz.ap[:, 0:256], [(gaT.ap[:, c0:c0 + 128], walb.ap[:, g, di, :])])], r=[gaT, walb])
            P.ACT(ez.ap, psz.ap[:, 0:256], AF.Exp, [psz], [ez], scale=-1.0)
            P.ACT(lz.ap, ez.ap, AF.Ln, [ez, cvec], [lz], bias=cvec.ap[:, 2:3], scale=1.0)
            psc = P.ps()
            pcv = psc.ap[0:64, :].rearrange("p (h t) -> p h t", h=4)
            P.mm(psc, [(pcv[:, h, :], [(lz.ap[:, h * 64:(h + 1) * 64], TRI[(d, 0)])]) for h in range(4)], r=[lz, cst])
            pss = P.ps()
            P.mm(pss, [(pss.ap[:, 0:256], [(TRI[(d, 1)], lz.ap)])], r=[lz, cst])
            P.ACT(E.ap, pcv, AF.Exp, [psc], [E])
            P.ACT(Es.ap, pss.ap[:, 0:256], AF.Exp, [pss], [Es])
            P.TT("dve", kh.ap, gktm.ap, Es.ap, ALU.mult, [gktm, Es], [kh])
            if full:
                P.ACT(En.ap, pcv, AF.Exp, [psc], [En], scale=-1.0)
                P.STT("dve", qt.ap, gqT.ap[:, :, c0:c0 + 128], 0.125, E.ap, ALU.mult, ALU.mult, [gqT, E], [qt])
                P.TT("dve", kt.ap, gkT.ap[:, :, c0:c0 + 128], En.ap, ALU.mult, [gkT, En], [kt])

        def o_op(ci, r0, gvb, Ab, qt):
            def fn(e):
                last = None
                for h in range(4):
                    if ci == 0:
                        last = e.matmul(pov[:, h, :], lhsT=gvb.ap[:, h * 128:(h + 1) * 128], rhs=Ab.ap[:, h, :], start=(h == 0), stop=False,
                                        skip_group_check=True)
                    last = e.matmul(pov[:, h, r0:r0 + 64], lhsT=Sb.ap[:, h, :], rhs=qt.ap[:, h, r0:r0 + 64], start=False, stop=(ci == 1),
                                    skip_group_check=True)
                return last
            P.S.op("pe", fn, [gvb, Ab, Sb, qt], [pso], name="o_op", cost=0.07 * 8, lat=0.1)

        def tile_scan(d, full, ocol):
            ts = tsets[tcnt[0] % 2]
            gvb, E, qt, kt, kh, Ab = ts.gvb, ts.E, ts.qt, ts.kt, ts.kh, ts.Ab
            if full:
                psa = P.ps()
                pav = psa.ap.rearrange("p (h t) -> p h t", h=4)
                P.mm(psa, [(pav[:, h, :], [(kt.ap[:, h, :], qt.ap[:, h, :])]) for h in range(4)], r=[kt, qt])
                P.TT("dve", Ab.ap, pav, MASK[d].unsqueeze(1).to_broadcast([128, 4, 128]), ALU.mult, [psa, cst], [Ab])
            chunks = (0, 1) if d == "f" else (1, 0)
            for ci, c in enumerate(chunks):
                r0 = c * 64
                if full:
                    o_op(ci, r0, gvb, Ab, qt)
                psn = P.ps()
                pnv = psn.ap[0:64, :].rearrange("p (h v) -> p h v", h=4)
                P.mm(psn, [(pnv[:, h, :], [(kh.ap[r0:r0 + 64, h * 64:(h + 1) * 64], gvb.ap[r0:r0 + 64, h * 128:(h + 1) * 128])])
                           for h in range(4)], r=[kh, gvb])
                col = (r0 + 63) if d == "f" else r0
                dec = E.ap[:, :, col:col + 1].to_broadcast([64, 4, 128])
                P.TT("dve", St.ap, Sst.ap, dec, ALU.mult, [Sst, E], [St])
                P.TT("dve", Sst.ap, St.ap, pnv, ALU.add, [St, psn], [Sst])
                P.CP("act", Sb.ap, Sst.ap, [Sst], [Sb])
            if full:
                if d == "b":
                    P.CP("act", ost.ap[:, :, ocol:ocol + 128], pov, [pso], [ost])
                else:
                    P.TT("dve", ost.ap[:, :, ocol:ocol + 128], pov, ost.ap[:, :, ocol:ocol + 128], ALU.add, [pso, ost], [ost])

        def finalize_block(job, b0, nt):
            TB = nt * 128
            for h in range(4):
                ov = ost.ap[:, h, b0 * 128:b0 * 128 + TB]
                P.ACT(osq.ap[:, 0:TB], ov, AF.Square, [ost], [osq])
                psb = P.ps()
                P.mm(psb, [(psb.ap[:, 0:TB], [(onesb.ap, osq.ap[:, 0:TB])])], r=[onesb, osq])
                P.rstd(psb, psb.ap[:, 0:TB], rsd, rsd.ap[:, 0:TB], 1.0 / 128.0, rt1, rt1.ap[:, 0:TB])
                P.TT("dve", rt1.ap[:, 0:TB], ov, rsd.ap[:, 0:TB], ALU.mult, [ost, rsd, rt1], [rt1])
                P.STT("dve", gl.ap[:, h, 0:TB], rt1.ap[:, 0:TB], gn_ap, sgg.ap[:, h, 0:TB], ALU.mult, ALU.mult, [rt1, fm, sgg], [gl])
            row = job["row0"] + b0 * 128
            P.store(GLT.ap[:, :, row:row + TB].rearrange("h p t -> p h t"), gl, eng="sp", src_ap=gl.ap[:, :, 0:TB], w=[GLT])

        def init_state(job, di):
            if job["g"] == 0:
                P.MEMSET("dve", Sst.ap, 0.0, [Sst])
            else:
                P.load(Sst, I["st0"][di].rearrange("h d v -> d h v"))
            P.CP("act", Sb.ap, Sst.ap, [Sst], [Sb])

        cur_g = None
        AB = None
        for job in jobs:
            g = job["g"]
            if cur_g != g:
                if AB is not None:
                    P.free(*AB)
                m = load_mod(0, g, [0, 1], "")
                make_A(m[1], I["v_nm"][0:1, :], "mod")
                AB = (m[1], m[0])
                cur_g = g
            A_, B_ = AB
            ext, other = job["ext"], job["other"]
            init_state(job, 1)
            ob = [other[i:i + 4] for i in range(0, len(other), 4)]
            for tiles in reversed(ob):
                block_prep(job, A_, B_, tiles, "b", False)
                for j in reversed(range(len(tiles))):
                    tile_prep(job, j, "b", False)
                    tile_scan("b", False, None)
            eb = [(i, ext[i:i + 4]) for i in range(0, len(ext), 4)]
            for (b0, tiles) in reversed(eb):
                block_prep(job, A_, B_, tiles, "b", True)
                for j in reversed(range(len(tiles))):
                    tile_prep(job, j, "b", True)
                    tile_scan("b", True, (b0 + j) * 128)
            if g == 0:
                P.store(O["st_o"][job["seq"], 1].rearrange("h d v -> d h v"), Sst, final=True)
            init_state(job, 0)
            for (b0, tiles) in eb:
                block_prep(job, A_, B_, tiles, "f", True)
                for j in range(len(tiles)):
                    tile_prep(job, j, "f", True)
                    tile_scan("f", True, (b0 + j) * 128)
                finalize_block(job, b0, len(tiles))
            if g == 0:
                P.store(O["st_o"][job["seq"], 0].rearrange("h d v -> d h v"), Sst, final=True)
        P.free(*AB)
        for t_ in tsets:
            P.free(t_.gktm, t_.gvb, t_.ez, t_.lz, t_.E, t_.En, t_.Es, t_.qt, t_.kt, t_.kh, t_.Ab)
        for bs_ in bsets_g:
            P.free(*bs_)
        P.free(w_gfm, w_gtm, walb, sgg, Sst, Sb, St, ost, osq, rsd, rt1, gl)

    S.phase = 'gla'
    gla_pass()

    def mk_ffn_bufs(pfx):
        wgs = [P.T(f"{pfx}wg{i}", [128, 8, 256], BF16) for i in range(3)]
        wus = [P.T(f"{pfx}wu{i}", [128, 8, 256], BF16) for i in range(3)]
        wds = [P.T(f"{pfx}wd{i}", [128, 2, 1024], BF16) for i in range(3)]
        sg2 = [P.T(f"{pfx}sg{i}", [128, 512]) for i in range(2)]
        return (wgs, wus, wds, sg2)

    def free_ffn_bufs(wb):
        P.free(*wb[0], *wb[1], *wb[2], *wb[3])

    def ffn(l, subs, hTb, actT, wb, tile_out):
        wgs, wus, wds, sg2 = wb
        for gi in range(NG):
            wg, wu = wgs[gi % 3], wus[gi % 3]
            P.load(wg, WG[l].ap[gi], r=[WG[l]])
            P.load(wu, WU[l].ap[gi], r=[WU[l]])
            for jj in range(2):
                jx = gi * 2 + jj
                for (c0, nt) in subs:
                    TB = nt * 128
                    pg, pu = P.ps(), P.ps()
                    P.mm(pg, [(pg.ap[:, 0:TB], [(wg.ap[:, c, jj * 128:(jj + 1) * 128], hTb.ap[:, c, c0:c0 + TB]) for c in range(8)])],
                         r=[wg, hTb])
                    P.mm(pu, [(pu.ap[:, 0:TB], [(wu.ap[:, c, jj * 128:(jj + 1) * 128], hTb.ap[:, c, c0:c0 + TB]) for c in range(8)])],
                         r=[wu, hTb])
                    sg = sg2[jx % 2]
                    P.ACT(sg.ap[:, 0:TB], pg.ap[:, 0:TB], AF.Silu, [pg], [sg])
                    P.TT("dve", actT.ap[:, jx, c0:c0 + TB], sg.ap[:, 0:TB], pu.ap[:, 0:TB], ALU.mult, [sg, pu], [actT])
        allps = S.psum_bufs
        wcount = 0
        for si, (c0, nt) in enumerate(subs):
            accs = [(allps[2 * j], allps[2 * j + 1]) for j in range(nt)]
            used = [b for pr in accs for b in pr]
            for gi in range(NG):
                wd = wds[wcount % 3]
                wcount += 1
                P.load(wd, WD[l].ap[gi], r=[WD[l]])
                groups = []
                for j in range(nt):
                    cc = c0 + j * 128
                    for hf in range(2):
                        pa = accs[j][hf].ap
                        pairs = [(actT.ap[:, gi * 2 + jj, cc:cc + 128], wd.ap[:, jj, hf * 512:(hf + 1) * 512]) for jj in range(2)]
                        groups.append((pa, pairs, gi == 0, gi == NG - 1))
                P.mm(used, groups, r=[actT, wd], name="ffn_down")
            for j in range(nt):
                tile_out(si, j, accs[j][0], accs[j][1])

    S.phase = 'ada1'
    ada_layer(1, 0)
    ada_layer(1, 1)

    def l0c_pass():
        w_out = P.T("w_out", [128, 8, D], BF16)
        P.load(w_out, I["w_out"], eng="pool")
        pw1 = P.T("pw1", [128, 8, 2 * D], BF16)
        P.load(pw1, I["pw1"], eng="pool")
        wb = mk_ffn_bufs("f_")
        wk = mk_wk("c_")
        MAXT = 5
        x1s = [P.T(f"xr_{i}", [128, D]) for i in range(MAXT)]
        hTb = P.T("c_hTb", [128, 8, MAXT * 128], BF16)
        actT = P.T("c_actT", [128, NJ, MAXT * 128], BF16)
        att = P.T("c_att", [128, 512], BF16)
        ccT = P.T("c_ccT", [128, 8, 128], BF16)
        tmp = wk["tmp"]
        usb = [P.T(f"c_u{i}", [128, 512]) for i in range(2)]
        sgu = [P.T(f"c_sgu{i}", [128, 512]) for i in range(2)]
        blocks = [(0, [(0, 4)]), (0, [(4, 4)])]
        if RUN_SAMPLE:
            for k in range(3):
                blocks.append((1, [(NPT + 4 * k, 4)]))
            blocks.append((1, [(NPT + 12, 4), (NPT + 16, 1)]))
        cur_g = None
        M = {}
        ucnt = 0
        for (g, sbs) in blocks:
            if cur_g != g:
                if M:
                    P.free(*M.values())
                    M = {}
                m0 = load_mod(0, g, [2, 3, 4, 5], "")
                make_A(m0[4], I["v_nf"][0:1, :], "mod")
                m1 = load_mod(1, g, [0, 1], "n")
                make_A(m1[1], I["v_nm"][1:2, :], "mod")
                M = dict(G1=m0[2], B2=m0[3], A2=m0[4], G2=m0[5], B1n=m1[0], A1n=m1[1])
                cur_g = g
            xsrc = I["xp"] if g == 0 else I["xs"]
            subs = []
            col = 0
            tl = []
            for (t0, nt) in sbs:
                subs.append((col, nt))
                for j in range(nt):
                    tl.append((t0 + j, col + j * 128))
                col += nt * 128
            for k, (st, cc) in enumerate(tl):
                x1 = x1s[k]
                lt = st if g == 0 else st - NPT
                P.load(x1, xsrc[lt * 128:(lt + 1) * 128, :])
                P.load(att, ATT.ap[st * 128:(st + 1) * 128, :], r=[ATT])
                pst = P.ps()
                pv = pst.ap.bitcast(BF16)[:, 0:512].rearrange("p (c t) -> p c t", c=4)
                P.tr(pst, [(pv[:, c, :], att.ap[:, c * 128:(c + 1) * 128]) for c in range(4)], r=[att, identb], ident=identb.ap)
                P.CP("act", ccT.ap[:, 0:4, :], pv, [pst], [ccT])
                P.load(ccT, GLT.ap[:, :, st * 128:(st + 1) * 128].rearrange("h p t -> p h t"), r=[GLT], dst_ap=ccT.ap[:, 4:8, :])
                p0, p1 = P.ps(), P.ps()
                P.mm(p0, [(p0.ap, [(ccT.ap[:, c, :], w_out.ap[:, c, 0:512]) for c in range(8)])], r=[ccT, w_out])
                P.mm(p1, [(p1.ap, [(ccT.ap[:, c, :], w_out.ap[:, c, 512:1024]) for c in range(8)])], r=[ccT, w_out])
                for hf, pp in ((0, p0), (1, p1)):
                    P.TT("dve", tmp.ap[:, hf * 512:(hf + 1) * 512], pp.ap, M["G1"].ap[:, hf * 512:(hf + 1) * 512], ALU.mult, [pp, M["G1"]], [tmp])
                P.TT("dve", x1.ap, x1.ap, tmp.ap, ALU.add, [x1, tmp], [x1])
                tm_norm_mod(x1, M["A2"], M["B2"], hTb, hTb.ap[:, :, cc:cc + 128], wk)

            def tile_out(si, j, p0, p1, subs=subs, sbs=sbs, g=g, M=M):
                k = sum(n for (_, n) in subs[:si]) + j
                x1 = x1s[k]
                st = sbs[si][0] + j
                for hf, pp in ((0, p0), (1, p1)):
                    P.TT("dve", tmp.ap[:, hf * 512:(hf + 1) * 512], pp.ap, M["G2"].ap[:, hf * 512:(hf + 1) * 512], ALU.mult, [pp, M["G2"]], [tmp])
                P.TT("dve", x1.ap, x1.ap, tmp.ap, ALU.add, [x1, tmp], [x1])
                if not (g == 1 and st == NPT + 16):
                    P.store(X2.ap[st * 128:(st + 1) * 128, :], x1, eng="sp", w=[X2])
            ffn(0, subs, hTb, actT, wb, tile_out)
            for k, (st, cc) in enumerate(tl):
                tm_norm_mod(x1s[k], M["A1n"], M["B1n"], hTb, hTb.ap[:, :, cc:cc + 128], wk)
            for (c0, nt), (t0, _) in zip(subs, sbs):
                TB = nt * 128
                for c in range(8):
                    pv_, pg_ = P.ps(), P.ps()
                    P.mm(pv_, [(pv_.ap[:, 0:TB], [(pw1.ap[:, k, c * 128:(c + 1) * 128], hTb.ap[:, k, c0:c0 + TB]) for k in range(8)])],
                         r=[pw1, hTb])
                    P.mm(pg_, [(pg_.ap[:, 0:TB], [(pw1.ap[:, k, D + c * 128:D + (c + 1) * 128], hTb.ap[:, k, c0:c0 + TB]) for k in range(8)])],
                         r=[pw1, hTb])
                    sg = sgu[ucnt % 2]
                    ub = usb[ucnt % 2]
                    ucnt += 1
                    P.ACT(sg.ap[:, 0:TB], pg_.ap[:, 0:TB], AF.Sigmoid, [pg_, fm], [sg], bias=fm.ap[:, 9 + c:10 + c], scale=1.0)
                    P.STT("dve", ub.ap[:, 0:TB], pv_.ap[:, 0:TB], fm.ap[:, 1 + c:2 + c], sg.ap[:, 0:TB], ALU.add, ALU.mult, [pv_, sg, fm], [ub])
                    P.store(UT.ap[c * 128:(c + 1) * 128, t0 * 128:t0 * 128 + TB], ub, eng="sp", src_ap=ub.ap[:, 0:TB], w=[UT])
        P.free(*M.values())
        free_wk(wk)
        free_ffn_bufs(wb)
        P.free(w_out, pw1, *x1s, hTb, actT, att, ccT, *usb, *sgu)

    S.phase = 'l0c'
    l0c_pass()

    def l1d_pass():
        pw2 = P.T("pw2", [128, 8, D], BF16)
        P.load(pw2, I["pw2"], eng="pool")
        cdw = P.T("cdw", [128, 2, 8, 31])
        P.load(cdw, I["cdw"])
        bpw2 = P.T("bpw2", [128, D])
        P.load(bpw2, I["vec"][0:1, 2048:3072].partition_broadcast(128))
        wb = mk_ffn_bufs("f_")
        wk = mk_wk("d_")
        x2s = [P.T(f"xr_{i}", [128, D]) for i in range(4)]
        hTb = P.T("d_hTb", [128, 8, 512], BF16)
        actT = P.T("d_actT", [128, NJ, 512], BF16)
        ups = [P.T(f"d_up{i}", [128, 512 + 32], BF16) for i in range(3)]
        dgs = [P.T(f"d_dg{i}", [128, 31, 128], BF16) for i in range(2)]
        upfs = [P.T(f"d_upf{i}", [128, 512 + 32]) for i in range(2)]
        fcnt = 0

        def conv_op(psb, pa, dg, up, ln, first):
            def fn(e):
                last = None
                for k in range(31):
                    last = e.matmul(pa, lhsT=dg.ap[:, k, :], rhs=up.ap[:, k:k + ln], start=(first and k == 0), stop=(k == 30),
                                    skip_group_check=True)
                return last
            P.S.op("pe", fn, [dg, up], [psb], name="conv", cost=31 * (ln / 2000.0 + 0.03), lat=0.1)
        cv = P.T("d_cv", [128, 8, 512])
        csqs = [P.T(f"d_csq{i}", [128, 512]) for i in range(2)]
        mu = P.T("d_mu", [128, 512])
        var = P.T("d_var", [128, 512])
        rsd = P.T("d_rsd", [128, 512])
        tts = [P.T(f"d_tt{i}", [128, 512]) for i in range(2)]
        onesf = P.T("d_onesf", [128, 128])
        P.MEMSET("dve", onesf.ap, 1.0 / D, [onesf])
        aT = P.T("d_aT", [128, 8, 512], BF16)
        tmp = wk["tmp"]
        blocks = []
        for k in range(2):
            blocks.append((0, 4 * k, [(0, 256, False, False), (256, 256, False, False)], O["yp"], 512 * k))
        if RUN_SAMPLE:
            for k in range(4):
                blocks.append((1, NPT + 4 * k, [(0, 512, k > 0, True)], O["ys"], 512 * k))
        cur_g = None
        M = {}
        ucnt = 0
        for (g, t0, pieces, ydst, yrow) in blocks:
            if cur_g != g:
                if M:
                    P.free(*M.values())
                    M = {}
                m1 = load_mod(1, g, [2, 3, 4, 5], "")
                make_A(m1[4], I["v_nf"][1:2, :], "mod")
                M = dict(G1=m1[2], B2=m1[3], A2=m1[4], G2=m1[5])
                cur_g = g
            base = t0 * 128
            for c in range(8):
                if c in DVE_CONV_CHUNKS:
                    for pi, (c0, ln, lh, rh) in enumerate(pieces):
                        upf = upfs[fcnt % 2]
                        fcnt += 1
                        lo = base + c0 - (15 if lh else 0)
                        hi = base + c0 + ln + (15 if rh else 0)
                        dlo = 0 if lh else 15
                        if not lh:
                            P.MEMSET("dve", upf.ap[:, 0:15], 0.0, [upf])
                        if not rh:
                            P.MEMSET("dve", upf.ap[:, 15 + ln:30 + ln], 0.0, [upf])
                        P.load(upf, UT.ap[c * 128:(c + 1) * 128, lo:hi], r=[UT], dst_ap=upf.ap[:, dlo:dlo + (hi - lo)])
                        dst = cv.ap[:, c, c0:c0 + ln]
                        P.TS("dve", dst, upf.ap[:, 0:ln], cdw.ap[:, g, c, 0:1], fm.ap[:, 17 + c:18 + c], ALU.mult, ALU.add, [upf, cdw, fm], [cv])
                        for k in range(1, 31):
                            P.STT("dve", dst, upf.ap[:, k:k + ln], cdw.ap[:, g, c, k:k + 1], dst, ALU.mult, ALU.add, [upf, cdw, cv], [cv])
                    continue
                dg = dgs[c % 2]
                P.TT("dve", dg.ap, identb.ap.unsqueeze(1).to_broadcast([128, 31, 128]),
                     cdw.ap[:, g, c, :].unsqueeze(2).to_broadcast([128, 31, 128]), ALU.mult, [identb, cdw], [dg])
                psc_ = P.ps()
                for pi, (c0, ln, lh, rh) in enumerate(pieces):
                    up = ups[ucnt % 3]
                    ucnt += 1
                    lo = base + c0 - (15 if lh else 0)
                    hi = base + c0 + ln + (15 if rh else 0)
                    dlo = 0 if lh else 15
                    if not lh:
                        P.MEMSET("dve", up.ap[:, 0:15], 0.0, [up])
                    if not rh:
                        P.MEMSET("dve", up.ap[:, 15 + ln:30 + ln], 0.0, [up])
                    P.load(up, UT.ap[c * 128:(c + 1) * 128, lo:hi], eng="pool", r=[UT], dst_ap=up.ap[:, dlo:dlo + (hi - lo)])
                    conv_op(psc_, psc_.ap[:, c0:c0 + ln], dg, up, ln, pi == 0)
                P.ACT(cv.ap[:, c, :], psc_.ap, AF.Identity, [psc_, fm], [cv], bias=fm.ap[:, 17 + c:18 + c], scale=1.0)
            psm, psq = P.ps(), P.ps()
            P.mm(psm, [(psm.ap, [(onesf.ap, cv.ap[:, c, :]) for c in range(8)])], r=[onesf, cv])
            for c in range(8):
                csq = csqs[c % 2]
                P.ACT(csq.ap, cv.ap[:, c, :], AF.Square, [cv], [csq])
                P.mm(psq, [(psq.ap, [(onesf.ap, csq.ap)], c == 0, c == 7)], r=[onesf, csq])
            P.CP("act", mu.ap, psm.ap, [psm], [mu])
            P.TT("dve", var.ap, mu.ap, mu.ap, ALU.mult, [mu], [var])
            P.TT("dve", var.ap, psq.ap, var.ap, ALU.subtract, [psq, var], [var])
            P.rstd(var, var.ap, rsd, rsd.ap, 1.0, tts[0], tts[0].ap)
            for c in range(8):
                tt = tts[c % 2]
                P.TT("dve", tt.ap, cv.ap[:, c, :], mu.ap, ALU.subtract, [cv, mu], [tt])
                P.TT("dve", tt.ap, tt.ap, rsd.ap, ALU.mult, [tt, rsd], [tt])
                P.ACT(aT.ap[:, c, :], tt.ap, AF.Silu, [tt, fm], [aT], scale=fm.ap[:, 25 + c:26 + c], bias=fm.ap[:, 33 + c:34 + c])
            for j in range(4):
                x2 = x2s[j]
                P.load(x2, X2.ap[base + j * 128: base + (j + 1) * 128, :], r=[X2])
                p0, p1 = P.ps(), P.ps()
                P.mm(p0, [(p0.ap, [(aT.ap[:, c, j * 128:(j + 1) * 128], pw2.ap[:, c, 0:512]) for c in range(8)])], r=[aT, pw2])
                P.mm(p1, [(p1.ap, [(aT.ap[:, c, j * 128:(j + 1) * 128], pw2.ap[:, c, 512:1024]) for c in range(8)])], r=[aT, pw2])
                for hf, pp in ((0, p0), (1, p1)):
                    P.TT("dve", tmp.ap[:, hf * 512:(hf + 1) * 512], pp.ap, bpw2.ap[:, hf * 512:(hf + 1) * 512], ALU.add, [pp, bpw2], [tmp])
                P.TT("dve", tmp.ap, tmp.ap, M["G1"].ap, ALU.mult, [tmp, M["G1"]], [tmp])
                P.TT("dve", x2.ap, x2.ap, tmp.ap, ALU.add, [x2, tmp], [x2])
                tm_norm_mod(x2, M["A2"], M["B2"], hTb, hTb.ap[:, :, j * 128:(j + 1) * 128], wk)

            def tile_out(si, j, p0, p1, ydst=ydst, yrow=yrow, M=M):
                x2 = x2s[j]
                for hf, pp in ((0, p0), (1, p1)):
                    P.TT("dve", tmp.ap[:, hf * 512:(hf + 1) * 512], pp.ap, M["G2"].ap[:, hf * 512:(hf + 1) * 512], ALU.mult, [pp, M["G2"]], [tmp])
                P.TT("dve", x2.ap, x2.ap, tmp.ap, ALU.add, [x2, tmp], [x2])
                P.store(ydst[yrow + j * 128: yrow + (j + 1) * 128, :], x2, eng="sp", final=True)
            ffn(1, [(0, 4)], hTb, actT, wb, tile_out)
        P.free(*M.values())
        free_wk(wk)
        free_ffn_bufs(wb)
        P.free(pw2, cdw, bpw2, *x2s, hTb, actT, *ups, cv, *csqs, mu, var, rsd, *tts, onesf, aT, *dgs, *upfs)

    S.phase = 'l1d'
    l1d_pass()

    fin = S.op("sp", lambda e: None, name="final")
    fin.deps.extend(P.outstores)
    return P


OUT_SPECS = {
    "yp": ([1024, D], F32), "ys": ([2048, D], F32), "ckv_o": ([1024, 256], F32), "kr_o": ([1024, 32], F32),
    "st_o": ([4, 2, 4, 64, 128], F32),
}


def build_nc(in_map0):
    nc = bass.Bass("TRN2", target_bir_lowering=False)
    I = {k: nc.dram_tensor(k, list(v.shape), F32, kind="ExternalInput").ap() for k, v in in_map0.items()}
    O = {k: nc.dram_tensor(k, list(sh), dt, kind="ExternalOutput").ap() for k, (sh, dt) in OUT_SPECS.items()}
    with ExitStack() as es:
        arena = es.enter_context(nc.sbuf_tensor("arena", [128, ARENA_WORDS], F32))
        S = Sched(nc, ARENA_WORDS)
        S.arena = arena[:]
        for i in range(8):
            pt = es.enter_context(nc.psum_tensor(f"ps{i}", [128, 512], F32))
            S.psum_bufs.append(Buf(f"ps{i}", pt[:], is_psum=True))
        build_program(nc, S, I, O)
        def sem_alloc(name):
            return es.enter_context(nc.semaphore(name))
        if RESCHED:
            S.reschedule()
        S.finalize_plan(sem_alloc)
        block = es.enter_context(nc.Block())

        @block.sync
        def _(e):
            S.run_engine("sp", e)

        @block.scalar
        def _(e):
            S.run_engine("act", e)

        @block.vector
        def _(e):
            S.run_engine("dve", e)

        @block.gpsimd
        def _(e):
            S.run_engine("pool", e)

        @block.tensor
        def _(e):
            S.run_engine("pe", e)
    return nc, S


def kernel(**inputs):
    inp = {k: np.asarray(v, dtype=np.float32) for k, v in inputs.items()}
    shared = prep_shared(inp)
    in_maps = []
    for core in range(8):
        d = dict(shared)
        d.update(prep_core(core, inp))
        in_maps.append(d)
    nc, S = build_nc(in_maps[0])
    res = run_bass_kernel_spmd(nc, in_maps, core_ids=list(range(8)))
    R = res.results
    y_prompt = np.concatenate([R[c]["yp"].reshape(4, 256, D) for c in range(8)], axis=0)
    y_sample = np.zeros((4, 4096, D), np.float32)
    for c in range(8):
        b, rev = c // 2, c % 2 == 1
        ys = R[c]["ys"]
        if rev:
            y_sample[b, 2048:] = ys[::-1]
        else:
            y_sample[b, :2048] = ys
    ckv = np.concatenate([R[c]["ckv_o"].reshape(4, 1, 256, 256) for c in range(8)], axis=0)
    kr = np.concatenate([R[c]["kr_o"].reshape(4, 1, 256, 32) for c in range(8)], axis=0)
    st = np.concatenate([R[c]["st_o"].reshape(4, 1, 2, 4, 64, 128) for c in range(8)], axis=0)
    return (y_prompt.astype(np.float32), y_sample, ckv.astype(np.float32), kr.astype(np.float32), st.astype(np.float32))
```
